# Optimizing a Trainium2 kernel written in Bass

```python
import math
import jax, jax.numpy as jnp
from jax import lax
import numpy as np

D_MODEL = 1024
BATCH = 4
SEQ = 8192
DEPTH = 4

PLE_DIM = 256
D_FF = 2816
D_RNN = D_MODEL
RNN_BLOCKS = 16
RNN_BW = D_RNN // RNN_BLOCKS
CONV_W = 4
RG_C = 8.0
N_HEADS = 8
HEAD_DIM = 128
D_ATTN = N_HEADS * HEAD_DIM
Q_BLOCK = 128
N_IN = 2 * D_RNN + 3 * D_ATTN + N_HEADS + 2 * D_MODEL
EPS = 1e-6

kernel_name = "hybrid_rglru_fox_macaron_ple"


def rms_norm(x, g):
    xf = x.astype(jnp.float32)
    y = xf * lax.rsqrt(jnp.mean(xf * xf, axis=-1, keepdims=True) + EPS)
    return (y * g.astype(jnp.float32)).astype(x.dtype)


def swiglu_ffn(h, w_in, w_out):
    g, u = jnp.split(h @ w_in, 2, axis=-1)
    return (jax.nn.silu(g) * u) @ w_out


def causal_dwconv(x, w, b):
    S = x.shape[1]
    xp = jnp.pad(x, ((0, 0), (CONV_W - 1, 0), (0, 0)))
    y = b + xp[:, 0:S] * w[0]
    for k in range(1, CONV_W):
        y = y + xp[:, k:k + S] * w[k]
    return y


def rg_lru(x, w_a, b_a, w_x, b_x, lam):
    B, S, C = x.shape
    xb = x.reshape(B, S, RNN_BLOCKS, RNN_BW)
    r = jax.nn.sigmoid(jnp.einsum('bsnc,ncd->bsnd', xb, w_a).reshape(B, S, C) + b_a)
    i = jax.nn.sigmoid(jnp.einsum('bsnc,ncd->bsnd', xb, w_x).reshape(B, S, C) + b_x)
    log_a = -RG_C * r.astype(jnp.float32) * jax.nn.softplus(-lam.astype(jnp.float32))
    a = jnp.exp(log_a)
    mult = jnp.sqrt(-jnp.expm1(2.0 * log_a))
    u = mult * (i * x).astype(jnp.float32)

    def combine(left, right):
        a1, b1 = left
        a2, b2 = right
        return a1 * a2, a2 * b1 + b2

    _, h = lax.associative_scan(combine, (a, u), axis=1)
    return h.astype(x.dtype)


def forgetting_attention(q, k, v, f_logit, f_b, q_g, k_g):
    B, S, _ = q.shape
    q = rms_norm(q.reshape(B, S, N_HEADS, HEAD_DIM), q_g)
    k = rms_norm(k.reshape(B, S, N_HEADS, HEAD_DIM), k_g)
    v = v.reshape(B, S, N_HEADS, HEAD_DIM)
    log_f = jax.nn.log_sigmoid((f_logit + f_b).astype(jnp.float32))
    dcum = jnp.cumsum(log_f, axis=1).transpose(0, 2, 1)
    qh = q.transpose(0, 2, 1, 3)
    kh = k.transpose(0, 2, 1, 3)
    vh = v.transpose(0, 2, 1, 3)
    nb = S // Q_BLOCK
    q_blocks = qh.reshape(B, N_HEADS, nb, Q_BLOCK, HEAD_DIM).transpose(2, 0, 1, 3, 4)
    d_blocks = dcum.reshape(B, N_HEADS, nb, Q_BLOCK).transpose(2, 0, 1, 3)
    kpos = jnp.arange(S)
    scale = 1.0 / math.sqrt(HEAD_DIM)

    def one_block(args):
        qb, dqb, blk = args
        s = jnp.einsum('bhqd,bhkd->bhqk', qb, kh).astype(jnp.float32) * scale
        s = s + dqb[..., None] - dcum[:, :, None, :]
        qpos = blk * Q_BLOCK + jnp.arange(Q_BLOCK)
        s = jnp.where(kpos[None, :] <= qpos[:, None], s, -jnp.inf)
        pr = jax.nn.softmax(s, axis=-1).astype(vh.dtype)
        return jnp.einsum('bhqk,bhkd->bhqd', pr, vh)

    o = lax.map(one_block, (q_blocks, d_blocks, jnp.arange(nb)))
    return o.transpose(1, 0, 3, 2, 4).reshape(B, S, D_ATTN)


def setup_inputs(seed: int = 0) -> dict:
    key = jax.random.key(seed)
    ks = jax.random.split(key, 32)
    f32 = jnp.float32

    def nrm(k, shape, fan_in):
        return jax.random.normal(k, shape, f32) * (fan_in ** -0.5)

    def gain(k, shape):
        return 1.0 + 0.05 * jax.random.normal(k, shape, f32)

    def small(k, shape):
        return 0.02 * jax.random.normal(k, shape, f32)

    a_c = jax.random.uniform(ks[12], (DEPTH, D_RNN), f32, 0.9, 0.999)
    s = a_c ** (1.0 / RG_C)
    rg_lambda = jnp.log(s) - jnp.log1p(-s)

    return {
        "x": jax.random.normal(ks[0], (BATCH, SEQ, D_MODEL), f32),
        "p": jax.random.normal(ks[1], (DEPTH, BATCH, SEQ, PLE_DIM), f32),
        "ffn1_norm": gain(ks[2], (DEPTH, D_MODEL)),
        "ffn1_w_in": nrm(ks[3], (DEPTH, D_MODEL, 2 * D_FF), D_MODEL),
        "ffn1_w_out": nrm(ks[4], (DEPTH, D_FF, D_MODEL), D_FF),
        "mix_norm": gain(ks[5], (DEPTH, D_MODEL)),
        "w_in": nrm(ks[6], (DEPTH, D_MODEL, N_IN), D_MODEL),
        "merge_b": small(ks[7], (DEPTH, 2 * D_MODEL)),
        "conv_w": nrm(ks[8], (DEPTH, CONV_W, D_RNN), CONV_W),
        "conv_b": small(ks[9], (DEPTH, D_RNN)),
        "rg_wa": nrm(ks[10], (DEPTH, RNN_BLOCKS, RNN_BW, RNN_BW), RNN_BW),
        "rg_ba": small(ks[11], (DEPTH, D_RNN)),
        "rg_wx": nrm(ks[13], (DEPTH, RNN_BLOCKS, RNN_BW, RNN_BW), RNN_BW),
        "rg_bx": small(ks[14], (DEPTH, D_RNN)),
        "rg_lambda": rg_lambda,
        "f_b": jax.random.uniform(ks[15], (DEPTH, N_HEADS), f32, 1.0, 4.0),
        "q_norm": gain(ks[16], (DEPTH, HEAD_DIM)),
        "k_norm": gain(ks[17], (DEPTH, HEAD_DIM)),
        "w_rnn_out": nrm(ks[18], (DEPTH, D_RNN, D_MODEL), D_RNN),
        "w_attn_out": nrm(ks[19], (DEPTH, D_ATTN, D_MODEL), D_ATTN),
        "w_o": nrm(ks[20], (DEPTH, D_MODEL, D_MODEL), D_MODEL),
        "ffn2_norm": gain(ks[21], (DEPTH, D_MODEL)),
        "ffn2_w_in": nrm(ks[22], (DEPTH, D_MODEL, 2 * D_FF), D_MODEL),
        "ffn2_w_out": nrm(ks[23], (DEPTH, D_FF, D_MODEL), D_FF),
        "ple_norm": gain(ks[24], (DEPTH, D_MODEL)),
        "ple_w_gate": nrm(ks[25], (DEPTH, D_MODEL, D_MODEL), D_MODEL),
        "ple_b_gate": small(ks[26], (DEPTH, D_MODEL)),
        "ple_w_proj": nrm(ks[27], (DEPTH, PLE_DIM, D_MODEL), PLE_DIM),
        "final_norm": gain(ks[28], (D_MODEL,)),
    }


def reference(x, p, ffn1_norm, ffn1_w_in, ffn1_w_out, mix_norm, w_in, merge_b,
              conv_w, conv_b, rg_wa, rg_ba, rg_wx, rg_bx, rg_lambda, f_b, q_norm, k_norm,
              w_rnn_out, w_attn_out, w_o, ffn2_norm, ffn2_w_in, ffn2_w_out,
              ple_norm, ple_w_gate, ple_b_gate, ple_w_proj, final_norm):
    split_idx = list(np.cumsum([D_RNN, D_RNN, D_ATTN, D_ATTN, D_ATTN, N_HEADS, D_MODEL]))
    for i in range(DEPTH):
        x = x + 0.5 * swiglu_ffn(rms_norm(x, ffn1_norm[i]), ffn1_w_in[i], ffn1_w_out[i])

        h = rms_norm(x, mix_norm[i])
        proj = h @ w_in[i]
        rx, rgate, q, k, v, f_logit, ga, gb = jnp.split(proj, split_idx, axis=-1)

        rx = causal_dwconv(rx, conv_w[i], conv_b[i])
        ya = rg_lru(rx, rg_wa[i], rg_ba[i], rg_wx[i], rg_bx[i], rg_lambda[i])
        ya = (ya * jax.nn.gelu(rgate)) @ w_rnn_out[i]

        yb = forgetting_attention(q, k, v, f_logit, f_b[i], q_norm[i], k_norm[i]) @ w_attn_out[i]

        mb_a, mb_b = jnp.split(merge_b[i], 2)
        merged = jax.nn.sigmoid(ga + mb_a) * ya + jax.nn.sigmoid(gb + mb_b) * yb
        x = x + merged @ w_o[i]

        x = x + 0.5 * swiglu_ffn(rms_norm(x, ffn2_norm[i]), ffn2_w_in[i], ffn2_w_out[i])

        gate = jax.nn.sigmoid(rms_norm(x, ple_norm[i]) @ ple_w_gate[i] + ple_b_gate[i])
        x = x + gate * (p[i] @ ple_w_proj[i])
    return rms_norm(x, final_norm)
```

```python
import math
from contextlib import ExitStack, contextmanager
import numpy as np
import concourse.bass as bass
import concourse.mybir as mybir
from concourse.bass_utils import run_bass_kernel_spmd

F32 = mybir.dt.float32
BF16 = mybir.dt.bfloat16
AF = mybir.ActivationFunctionType
ALU = mybir.AluOpType

D = 1024
DFF = 2816
NF = DFF // 128
NH = 8
PLE = 256
EPS = 1e-6
TT = 512
CASTW = 8192


VEC_NAMES = [("ffn1_norm", 8), ("mix_norm", 8), ("conv_w", 32), ("conv_b", 8), ("rg_ba", 8), ("rg_bx", 8),
             ("rg_lambda", 8), ("merge_b", 16), ("q_norm", 1), ("k_norm", 1), ("ffn2_norm", 8),
             ("ple_norm", 8), ("ple_b_gate", 8), ("f_b", 1)]
VOFF = {}
_o = 0
for _n, _w in VEC_NAMES:
    VOFF[_n] = _o
    _o += _w
NVL = _o


def pc(v):
    return np.ascontiguousarray(v.reshape(-1, 128).T)


def kxn(w):
    K, n = w.shape
    return np.ascontiguousarray(w.reshape(K // 128, 128, n).transpose(1, 0, 2)).reshape(128, -1)


def ffn_blocks(w_in, w_out):
    out = []
    for f in range(NF):
        g = kxn(w_in[:, f * 128:(f + 1) * 128])
        u = kxn(w_in[:, DFF + f * 128:DFF + (f + 1) * 128])
        out.append(np.concatenate([g, u], axis=1))
    for d in range(8):
        out.append(kxn(w_out[:, d * 128:(d + 1) * 128]))
    return out


def layer_blocks(inp, l):
    blocks = ffn_blocks(inp["ffn1_w_in"][l], inp["ffn1_w_out"][l])
    rg = np.zeros((128, 8, 2, 128), np.float32)
    for c in range(8):
        for s in range(2):
            n = 2 * c + s
            rg[s * 64:(s + 1) * 64, c, 0, s * 64:(s + 1) * 64] = inp["rg_wa"][l, n]
            rg[s * 64:(s + 1) * 64, c, 1, s * 64:(s + 1) * 64] = inp["rg_wx"][l, n]
    blocks.append(rg.reshape(128, -1))
    W = inp["w_in"][l]
    for c in range(8):
        blocks.append(np.concatenate([kxn(W[:, c * 128:(c + 1) * 128]),
                                      kxn(W[:, 1024 + c * 128:1024 + (c + 1) * 128])], axis=1))
    for h in range(8):
        blocks.append(np.concatenate([kxn(W[:, 2048 + h * 128:2048 + (h + 1) * 128]),
                                      kxn(W[:, 3072 + h * 128:3072 + (h + 1) * 128])], axis=1))
    for qv in range(4):
        blocks.append(kxn(W[:, 4096 + qv * 256:4096 + (qv + 1) * 256]))
    blocks.append(kxn(W[:, 5120:5128]))
    for c in range(8):
        blocks.append(np.concatenate([kxn(W[:, 5128 + c * 128:5128 + (c + 1) * 128]),
                                      kxn(W[:, 6152 + c * 128:6152 + (c + 1) * 128])], axis=1))
    for d in range(8):
        blocks.append(np.concatenate([kxn(inp["w_rnn_out"][l][:, d * 128:(d + 1) * 128]),
                                      kxn(inp["w_attn_out"][l][:, d * 128:(d + 1) * 128])], axis=1))
    for dp in range(4):
        blocks.append(np.concatenate([kxn(inp["w_o"][l][:, (2 * dp) * 128:(2 * dp + 1) * 128]),
                                      kxn(inp["w_o"][l][:, (2 * dp + 1) * 128:(2 * dp + 2) * 128])], axis=1))
    blocks += ffn_blocks(inp["ffn2_w_in"][l], inp["ffn2_w_out"][l])
    for dp in range(4):
        blocks.append(np.concatenate([kxn(inp["ple_w_gate"][l][:, (2 * dp) * 128:(2 * dp + 1) * 128]),
                                      kxn(inp["ple_w_gate"][l][:, (2 * dp + 1) * 128:(2 * dp + 2) * 128])], axis=1))
        blocks.append(np.concatenate([kxn(inp["ple_w_proj"][l][:, (2 * dp) * 128:(2 * dp + 1) * 128]),
                                      kxn(inp["ple_w_proj"][l][:, (2 * dp + 1) * 128:(2 * dp + 2) * 128])], axis=1))
    return blocks


BLOCK_E = ([2048] * NF + [2816] * 8 + [2048] + [2048] * 8 + [2048] * 8 + [2048] * 4 + [64] + [2048] * 8
           + [2048] * 8 + [2048] * 4 + [2048] * NF + [2816] * 8 + [2048, 512] * 4)
N_A = NF + 8 + 1 + 8 + 8 + 4 + 1 + 8
EL = sum(BLOCK_E)
ELP = ((EL + CASTW - 1) // CASTW) * CASTW
BLOCK_OFF = np.concatenate([[0], np.cumsum(BLOCK_E)]).astype(int)


def prep_shared(inp, L):
    ws = np.zeros((L, 128, ELP), np.float32)
    vecs = np.zeros((128, L * NVL + 8), np.float32)
    for l in range(L):
        bl = layer_blocks(inp, l)
        assert [b.shape[1] for b in bl] == BLOCK_E
        ws[l, :, :EL] = np.concatenate(bl, axis=1)
        o = l * NVL
        for n, w in VEC_NAMES:
            if n == "conv_w":
                v = np.concatenate([pc(inp["conv_w"][l, k]) for k in range(4)], axis=1)
            elif n == "f_b":
                v = np.zeros((128, 1), np.float32)
                v[:8, 0] = inp["f_b"][l]
            else:
                v = pc(inp[n][l])
            vecs[:, o + VOFF[n]:o + VOFF[n] + w] = v
    vecs[:, L * NVL:L * NVL + 8] = pc(inp["final_norm"])
    mask = np.zeros((4, 128, 512), np.float32)
    for r in range(4):
        p = np.arange(128)[:, None]
        c = np.arange(512)[None, :]
        mask[r] = np.where(128 * r + p <= c, 0.0, -1e30)
    negi = -np.eye(8, dtype=np.float32)
    ones = np.ones((128, 128), np.float32)
    return {"ws": ws, "vecs": vecs, "mask": mask, "negi": negi, "ones": ones}


class KB:
    def __init__(self, nc, es):
        self.nc, self.es = nc, es
        self.eng = {"PE": nc.tensor, "ACT": nc.scalar, "DVE": nc.vector, "POOL": nc.gpsimd, "SP": nc.sync}
        self.sem = {}
        self.count = {}
        self.lastw = {}
        self.readers = {}
        self.waited = {}
        self.nbar = 0
        for n in ("PE", "ACT", "DVE", "POOL"):
            self._counter(n)

    def _counter(self, name):
        if name not in self.sem:
            self.sem[name] = self.es.enter_context(self.nc.semaphore("c%d" % len(self.sem)))
            self.count[name] = 0
        return self.sem[name]

    @staticmethod
    def _isdram(k):
        return isinstance(k, tuple) and k[0] == "D"

    def _deps(self, en, own, r, w):
        need = {}
        for b in r:
            for cn, v in self.lastw.get(b, {}).items():
                need[cn] = max(need.get(cn, 0), v)
        for b in w:
            for cn, v in self.lastw.get(b, {}).items():
                need[cn] = max(need.get(cn, 0), v)
            for cn, v in self.readers.get(b, {}).items():
                need[cn] = max(need.get(cn, 0), v)
        for cn, v in need.items():
            if cn == own and own == "PE":
                continue
            if self.waited.get((en, cn), 0) >= v:
                continue
            self.waited[(en, cn)] = v
            self.eng[en].wait_ge(self.sem[cn], v)

    def _record(self, cn, r, w):
        val = self.count[cn]
        for b in w:
            if self._isdram(b):
                self.lastw.setdefault(b, {})[cn] = val
            else:
                self.lastw[b] = {cn: val}
                self.readers[b] = {}
        for b in r:
            self.readers.setdefault(b, {})[cn] = val

    def op(self, en, fn, r=(), w=()):
        self._deps(en, en, r, w)
        ins = fn(self.eng[en])
        ins.then_inc(self.sem[en], 1)
        self.count[en] += 1
        self._record(en, r, w)
        return ins

    def dma(self, en, fn, r=(), w=()):
        sb_w = [k for k in w if not self._isdram(k)]
        sb_r = [k for k in r if not self._isdram(k)]
        if sb_w:
            cn = "L:" + repr(sb_w[0])
        elif sb_r:
            cn = "S:" + repr(sb_r[0])
        else:
            cn = "misc"
        self._counter(cn)
        self._deps(en, None, r, w)
        ins = fn(self.eng[en])
        ins.then_inc(self.sem[cn], 16)
        self.count[cn] += 16
        self._record(cn, r, w)
        return ins

    def softbar(self):
        for en in self.eng:
            for cn, v in self.count.items():
                if cn == en or v <= 0 or self.waited.get((en, cn), 0) >= v:
                    continue
                self.waited[(en, cn)] = v
                self.eng[en].wait_ge(self.sem[cn], v)

    def reset(self):
        self.softbar()
        for n in ("PE", "ACT", "DVE", "POOL"):
            if self.count[n] > 0:
                self.sem[n] = self.es.enter_context(self.nc.semaphore("c%s_%d" % (n, self.nbar)))
                self.count[n] = 0
        self.nbar += 1
        self.lastw, self.readers = {}, {}
        self.waited = {k: v for k, v in self.waited.items() if k[1] not in ("PE", "ACT", "DVE", "POOL")}


def tsl_(i, n):
    return slice(i * n, (i + 1) * n)


NBW = 3


def build_program(S, L, stop=None, debug=False):
    NT = S // TT
    NKT = S // 128
    nc = bass.Bass("TRN2", target_bir_lowering=False)
    xin = nc.dram_tensor("xT", [D, S], F32, kind="ExternalInput").ap()
    pin = nc.dram_tensor("pT", [L, PLE, S], F32, kind="ExternalInput").ap()
    ws = nc.dram_tensor("ws", [L, 128, ELP], F32, kind="ExternalInput").ap()
    vecs_d = nc.dram_tensor("vecs", [128, L * NVL + 8], F32, kind="ExternalInput").ap()
    mask_d = nc.dram_tensor("mask", [4, 128, 512], F32, kind="ExternalInput").ap()
    negi_d = nc.dram_tensor("negi", [8, 8], F32, kind="ExternalInput").ap()
    ones_d = nc.dram_tensor("ones", [128, 128], F32, kind="ExternalInput").ap()
    outT = nc.dram_tensor("outT", [D, S], F32, kind="ExternalOutput").ap()
    wbf = nc.dram_tensor("wbf", [L, 128, ELP], BF16).ap()
    kw = dict(kind="ExternalOutput") if debug else {}
    xs = nc.dram_tensor("xs", [D, S], F32, **kw).ap()
    yrT = nc.dram_tensor("yrT", [D, S], BF16, **kw).ap()
    atT = nc.dram_tensor("atT", [D, S], BF16, **kw).ap()
    sgT = nc.dram_tensor("sgT", [2 * D, S], BF16, **kw).ap()
    qT = nc.dram_tensor("qT", [NH, 128, S], BF16, **kw).ap()
    kT = nc.dram_tensor("kT", [NH, 128, S], BF16, **kw).ap()
    vv = nc.dram_tensor("vv", [S, D], BF16, **kw).ap()
    Dd = nc.dram_tensor("Dd", [NH, S], F32, **kw).ap()
    xdbg = [nc.dram_tensor("xd%d" % i, [D, S], F32, **kw).ap() for i in range(3)] if debug else None

    es = ExitStack()

    def sb(name, shape, dt):
        return es.enter_context(nc.sbuf_tensor(name, shape, dt))

    with es:
        X = sb("X", [128, 8, TT], F32)
        H = sb("H", [128, 8, TT], BF16)
        AFt = sb("AF", [128, NF, TT], BF16)
        SQ = sb("SQ", [128, 8, TT], BF16)
        WBall = sb("WB", [128, NBW, 2816], BF16)
        WB = [WBall[:, i, :] for i in range(NBW)]
        VEC = sb("VEC", [128, L * NVL + 8], F32)
        CL = sb("CL", [128, L * 8], F32)
        NFB = sb("NFB", [128, L], F32)
        TMPV = sb("TMPV", [128, L * 8], F32)
        NEGI = sb("NEGI", [8, 8], F32)
        ONES = sb("ONES", [128, 128], BF16)
        ONE8 = sb("ONE8", [8, TT], F32)
        RSTD = sb("RSTD", [128, 2, TT], F32)
        STD = sb("STD", [128, 2, TT], F32)
        SG = sb("SG", [128, 2, TT], F32)
        HL = sb("HL", [128, 8], F32)
        DL = sb("DL", [8, 1], F32)
        OB = sb("OB", [128, 4, TT], BF16)
        FE = sb("FE", [8, TT], F32)
        FD = sb("FD", [8, TT], F32)
        NDK = sb("NDK", [128, NKT, 8], F32)
        NDKH = sb("NDKH", [128, NKT], F32)
        NFA = 8 * (TT + 3) + 18 * TT
        FA = sb("FA", [128, NFA], F32)
        RXH = FA[:, 0:8 * (TT + 3)].rearrange("p (c t) -> p c t", t=TT + 3)
        _fo = 8 * (TT + 3)

        def fa2(i):
            return FA[:, _fo + i * 2 * TT:_fo + (i + 1) * 2 * TT].rearrange("p (b t) -> p b t", t=TT)
        T1, XC, RR, II, AA, MM, UU, HS, GL = [fa2(i) for i in range(9)]
        MASK = FA[:, 0:4 * TT].rearrange("p (r t) -> p r t", t=TT)
        M1, M2, OF = fa2(0), fa2(1), fa2(2)
        CB = sb("CB", [128, 42, TT], BF16)
        RGWt = CB[:, 0:4, :].rearrange("p f t -> p (f t)")
        XCB = CB[:, 4:6, :]
        VO = sb("VO", [128, 2, 256], BF16)
        YRt, ATt, SGt, PTt, MG = CB[:, 0:8, :], CB[:, 8:16, :], CB[:, 16:32, :], CB[:, 32:34, :], CB[:, 34:42, :]
        KTb = AFt[:, :, :].rearrange("p f t -> p (f t)")[:, 0:S]
        VHb = WBall[:, :, :].rearrange("p f t -> p (f t)")[:, 0:NKT * 128].rearrange("p (j d) -> p j d", d=128)
        QTb, PPb = H[:, 0:2, :], H[:, 2:4, :]
        KT4 = KTb.rearrange("p (n f d) -> p n f d", f=4, d=128)
        VH4 = VHb.rearrange("p (n f) d -> p n f d", f=4)
        NDKH4 = NDKH[:, :].rearrange("p (n f) -> p n f", f=4)
        DQb, TTb, RLt = X[:, 0:2, :], X[:, 2:4, :], X[:, 4:6, :]
        PS = es.enter_context(nc.psum_tensor("PS", [128, 8, TT], F32))
        es.enter_context(nc.Block())
        kb = KB(nc, es)
        op, dma = kb.op, kb.dma

        rot = {}

        def nxt(name, n):
            i = rot.get(name, 0)
            rot[name] = (i + 1) % n
            return i

        def bank():
            b = nxt("ps", 8)
            return ("ps", b), PS[:, b, :]

        ncast_l = ELP // CASTW
        for l in range(L):
            for c in range(ncast_l):
                dma("POOL", lambda e, l=l, c=c: e.dma_start(out=wbf[l, :, c * CASTW:(c + 1) * CASTW],
                                                             in_=ws[l, :, c * CASTW:(c + 1) * CASTW]), w=[("D", "wbf")])
        dma("SP", lambda e: e.dma_start(out=VEC[:], in_=vecs_d[:, :]), w=["VEC"])
        dma("SP", lambda e: e.dma_start(out=NEGI[:], in_=negi_d[:, :]), w=["NEGI"])
        dma("POOL", lambda e: e.dma_start(out=ONES[:], in_=ones_d[:, :]), w=["ONES"])
        op("DVE", lambda e: e.memset(ONE8[:], 1.0), w=["ONE8"])
        op("DVE", lambda e: e.memset(FA[:], 0.0), w=["RXHall"])
        for l in range(L):
            o = l * NVL
            lc = o + VOFF["rg_lambda"]
            op("ACT", lambda e: e.activation(out=TMPV[:, l * 8:(l + 1) * 8], in_=VEC[:, lc:lc + 8], func=AF.Exp, scale=-1.0),
               r=["VEC"], w=["TMPV"])
            op("ACT", lambda e: e.activation(out=TMPV[:, l * 8:(l + 1) * 8], in_=TMPV[:, l * 8:(l + 1) * 8], func=AF.Ln, bias=1.0),
               r=["TMPV"], w=["TMPV"])
            op("ACT", lambda e: e.mul(out=CL[:, l * 8:(l + 1) * 8], in_=TMPV[:, l * 8:(l + 1) * 8], mul=-8.0), r=["TMPV"], w=["CL"])
            fc = o + VOFF["f_b"]
            op("ACT", lambda e: e.mul(out=NFB[:, l:l + 1], in_=VEC[:, fc:fc + 1], mul=-1.0), r=["VEC"], w=["NFB"])
        kb.reset()

        wst = {"seq": [], "issued": 0, "used": 0, "l": 0}

        def wbegin(l, first, count):
            wst.update(seq=list(range(first, first + count)), issued=0, used=0, l=l)

        def wnext(E):
            seq = wst["seq"]
            while wst["issued"] < min(wst["used"] + NBW, len(seq)):
                b = seq[wst["issued"]]
                sl = wst["issued"] % NBW
                Eb = BLOCK_E[b]
                off = int(BLOCK_OFF[b])
                l = wst["l"]
                dma("SP", lambda e, sl=sl, Eb=Eb, off=off, l=l: e.dma_start(out=WB[sl][:, 0:Eb], in_=wbf[l, :, off:off + Eb]),
                    r=[("D", "wbf")], w=[("W", sl)])
                wst["issued"] += 1
            b = seq[wst["used"]]
            assert BLOCK_E[b] == E, (b, BLOCK_E[b], E)
            sl = wst["used"] % NBW
            wst["used"] += 1
            return ("W", sl), WB[sl]

        def wend():
            assert wst["used"] == len(wst["seq"]), (wst["used"], len(wst["seq"]))

        def mm_group(pk, pp, lhs_list, rhs_list, rk):
            n = len(lhs_list)
            for k in range(n):
                ins = op("PE", lambda e, k=k: e.matmul(pp, lhsT=lhs_list[k], rhs=rhs_list[k], start=(k == 0), stop=(k == n - 1)),
                         r=rk[k], w=[pk])
            return ins

        def rms(gcol, out_final=None, tsl=None):
            for c in range(8):
                op("ACT", lambda e, c=c: e.activation(out=SQ[:, c, :], in_=X[:, c, :], func=AF.Square), r=[("X", c)], w=[("SQ", c)])
            qk_, qp = bank()
            mm_group(qk_, qp, [ONES[:]] * 8, [SQ[:, c, :] for c in range(8)], [[("SQ", c), "ONES"] for c in range(8)])
            s = nxt("std", 2)
            op("ACT", lambda e: e.activation(out=STD[:, s, :], in_=qp, func=AF.Sqrt, scale=1.0 / D, bias=EPS), r=[qk_], w=[("STD", s)])
            op("DVE", lambda e: e.reciprocal(out=RSTD[:, s, :], in_=STD[:, s, :]), r=[("STD", s)], w=[("RSTD", s)])
            for c in range(8):
                if out_final is None:
                    op("DVE", lambda e, c=c: e.scalar_tensor_tensor(out=H[:, c, :], in0=X[:, c, :], scalar=VEC[:, gcol + c:gcol + c + 1],
                                                                    in1=RSTD[:, s, :], op0=ALU.mult, op1=ALU.mult),
                       r=[("X", c), ("RSTD", s)], w=[("H", c)])
                else:
                    o2 = nxt("of", 2)
                    op("DVE", lambda e, c=c: e.scalar_tensor_tensor(out=OF[:, o2, :], in0=X[:, c, :], scalar=VEC[:, gcol + c:gcol + c + 1],
                                                                    in1=RSTD[:, s, :], op0=ALU.mult, op1=ALU.mult),
                       r=[("X", c), ("RSTD", s)], w=[("OF", o2)])
                    dma("POOL", lambda e, c=c: e.dma_start(out=outT[c * 128:(c + 1) * 128, tsl], in_=OF[:, o2, :]), r=[("OF", o2)], w=[("D", "outT")])

        def ffn():
            for f in range(NF):
                wk, wb = wnext(2048)
                gk, gp = bank()
                mm_group(gk, gp, [wb[:, k * 128:(k + 1) * 128] for k in range(8)], [H[:, k, :] for k in range(8)],
                         [[wk, ("H", k)] for k in range(8)])
                uk, up = bank()
                mm_group(uk, up, [wb[:, 1024 + k * 128:1024 + (k + 1) * 128] for k in range(8)], [H[:, k, :] for k in range(8)],
                         [[wk, ("H", k)] for k in range(8)])
                s = nxt("sg", 2)
                op("ACT", lambda e: e.activation(out=SG[:, s, :], in_=gp, func=AF.Silu), r=[gk], w=[("SG", s)])
                op("DVE", lambda e: e.tensor_tensor(out=AFt[:, f, :], in0=up, in1=SG[:, s, :], op=ALU.mult), r=[uk, ("SG", s)], w=[("AF", f)])
            for d in range(8):
                wk, wb = wnext(2816)
                yk, yp = bank()
                mm_group(yk, yp, [wb[:, f * 128:(f + 1) * 128] for f in range(NF)], [AFt[:, f, :] for f in range(NF)],
                         [[wk, ("AF", f)] for f in range(NF)])
                op("DVE", lambda e: e.scalar_tensor_tensor(out=X[:, d, :], in0=yp, scalar=0.5, in1=X[:, d, :], op0=ALU.mult, op1=ALU.add),
                   r=[yk, ("X", d)], w=[("X", d)])

        def proj_chunk(wk, wb, woff, nk=8, rhs=None, rkeys=None):
            pk, pp = bank()
            rhs = rhs or [H[:, k, :] for k in range(nk)]
            rkeys = rkeys or [("H", k) for k in range(nk)]
            mm_group(pk, pp, [wb[:, woff + k * 128:woff + (k + 1) * 128] for k in range(nk)], rhs, [[wk, rkeys[k]] for k in range(nk)])
            return pk, pp

        xview = lambda t: t.rearrange("(c p) t -> p c t", p=128)

        for l in range(L):
            vo = l * NVL
            xsrc = xin if l == 0 else xs
            op("DVE", lambda e: e.memset(HL[:], 0.0), w=["HL"])
            op("DVE", lambda e: e.memset(DL[:], 0.0), w=["DL"])
            op("DVE", lambda e: e.memset(RXH[:, :, 0:3], 0.0), w=["RXHall"])
            for it in range(NT):
                tsl = tsl_(it, TT)
                wbegin(l, 0, N_A)
                dma("SP", lambda e: e.dma_start(out=X[:], in_=xview(xsrc)[:, :, tsl]), r=[("D", "xs")], w=[("X", c) for c in range(8)])
                rms(vo + VOFF["ffn1_norm"])
                ffn()
                dma("POOL", lambda e: e.dma_start(out=xview(xs)[:, :, tsl], in_=X[:]), r=[("X", c) for c in range(8)], w=[("D", "xs")])
                rms(vo + VOFF["mix_norm"])
                wk, wb = wnext(2048)
                op("DVE", lambda e: e.tensor_copy(out=RGWt[:], in_=wb[:, 0:2048]), r=[wk], w=["RGW"])
                cw = vo + VOFF["conv_w"]
                cbc = vo + VOFF["conv_b"]
                for c in range(8):
                    b2 = c % 2
                    wk, wb = wnext(2048)
                    rxk, rxp = proj_chunk(wk, wb, 0)
                    gtk, gtp = proj_chunk(wk, wb, 1024)
                    op("ACT", lambda e: e.activation(out=RXH[:, c, 3:TT + 3], in_=rxp, func=AF.Copy), r=[rxk], w=[("RXH", c)])
                    op("DVE", lambda e: e.tensor_scalar(out=T1[:, b2, :], in0=RXH[:, c, 0:TT], scalar1=VEC[:, cw + c:cw + c + 1],
                                                        scalar2=VEC[:, cbc + c:cbc + c + 1], op0=ALU.mult, op1=ALU.add),
                       r=[("RXH", c)], w=[("T1", b2)])
                    for kk in (1, 2):
                        op("DVE", lambda e, kk=kk: e.scalar_tensor_tensor(out=T1[:, b2, :], in0=RXH[:, c, kk:TT + kk],
                                                                          scalar=VEC[:, cw + 8 * kk + c:cw + 8 * kk + c + 1],
                                                                          in1=T1[:, b2, :], op0=ALU.mult, op1=ALU.add),
                           r=[("RXH", c)], w=[("T1", b2)])
                    op("DVE", lambda e: e.scalar_tensor_tensor(out=XC[:, b2, :], in0=RXH[:, c, 3:TT + 3], scalar=VEC[:, cw + 24 + c:cw + 25 + c],
                                                               in1=T1[:, b2, :], op0=ALU.mult, op1=ALU.add),
                       r=[("RXH", c), ("T1", b2)], w=[("XC", b2)])
                    op("DVE", lambda e: e.tensor_copy(out=RXH[:, c, 0:3], in_=RXH[:, c, TT:TT + 3]), r=[("RXH", c)], w=[("RXH", c)])
                    op("DVE", lambda e: e.tensor_copy(out=XCB[:, b2, :], in_=XC[:, b2, :]), r=[("XC", b2)], w=[("XCB", b2)])
                    rk_, rp = bank()
                    op("PE", lambda e: e.matmul(rp, lhsT=RGWt[:, c * 256:c * 256 + 128], rhs=XCB[:, b2, :], start=True, stop=True),
                       r=["RGW", ("XCB", b2)], w=[rk_])
                    ik_, ip = bank()
                    op("PE", lambda e: e.matmul(ip, lhsT=RGWt[:, c * 256 + 128:c * 256 + 256], rhs=XCB[:, b2, :], start=True, stop=True),
                       r=["RGW", ("XCB", b2)], w=[ik_])
                    bac = vo + VOFF["rg_ba"] + c
                    bxc = vo + VOFF["rg_bx"] + c
                    op("ACT", lambda e: e.activation(out=RR[:, b2, :], in_=rp, func=AF.Sigmoid, bias=VEC[:, bac:bac + 1]), r=[rk_], w=[("RR", b2)])
                    op("ACT", lambda e: e.activation(out=AA[:, b2, :], in_=RR[:, b2, :], func=AF.Exp, scale=CL[:, l * 8 + c:l * 8 + c + 1]),
                       r=[("RR", b2)], w=[("AA", b2)])
                    op("ACT", lambda e: e.activation(out=II[:, b2, :], in_=ip, func=AF.Sigmoid, bias=VEC[:, bxc:bxc + 1]), r=[ik_], w=[("II", b2)])
                    op("ACT", lambda e: e.activation(out=MM[:, b2, :], in_=AA[:, b2, :], func=AF.Square), r=[("AA", b2)], w=[("MM", b2)])
                    op("ACT", lambda e: e.activation(out=MM[:, b2, :], in_=MM[:, b2, :], func=AF.Sqrt, scale=-1.0, bias=1.0), r=[("MM", b2)], w=[("MM", b2)])
                    op("ACT", lambda e: e.activation(out=GL[:, b2, :], in_=gtp, func=AF.Gelu_apprx_tanh), r=[gtk], w=[("GL", b2)])
                    op("DVE", lambda e: e.tensor_tensor(out=UU[:, b2, :], in0=II[:, b2, :], in1=XC[:, b2, :], op=ALU.mult),
                       r=[("II", b2), ("XC", b2)], w=[("UU", b2)])
                    op("DVE", lambda e: e.tensor_tensor(out=UU[:, b2, :], in0=UU[:, b2, :], in1=MM[:, b2, :], op=ALU.mult),
                       r=[("MM", b2)], w=[("UU", b2)])
                    op("DVE", lambda e: e.scalar_tensor_tensor(out=UU[:, b2, 0:1], in0=AA[:, b2, 0:1], scalar=HL[:, c:c + 1], in1=UU[:, b2, 0:1],
                                                               op0=ALU.mult, op1=ALU.add),
                       r=[("AA", b2), "HL"], w=[("UU", b2)])
                    op("DVE", lambda e: e.tensor_tensor_scan(out=HS[:, b2, :], data0=AA[:, b2, :], data1=UU[:, b2, :], initial=0.0,
                                                             op0=ALU.mult, op1=ALU.add),
                       r=[("AA", b2), ("UU", b2)], w=[("HS", b2)])
                    op("DVE", lambda e: e.tensor_copy(out=HL[:, c:c + 1], in_=HS[:, b2, TT - 1:TT]), r=[("HS", b2)], w=["HL"])
                    o4 = nxt("ob", 4)
                    op("DVE", lambda e: e.tensor_tensor(out=OB[:, o4, :], in0=HS[:, b2, :], in1=GL[:, b2, :], op=ALU.mult),
                       r=[("HS", b2), ("GL", b2)], w=[("OB", o4)])
                    dma("POOL", lambda e: e.dma_start(out=yrT[c * 128:(c + 1) * 128, tsl], in_=OB[:, o4, :]), r=[("OB", o4)], w=[("D", "yrT")])
                for h in range(NH):
                    wk, wb = wnext(2048)
                    for qk in range(2):
                        pk, pp = proj_chunk(wk, wb, qk * 1024)
                        op("ACT", lambda e: e.activation(out=SQ[:, qk, :], in_=pp, func=AF.Square), r=[pk], w=[("SQ", qk)])
                        sk_, sp_ = bank()
                        op("PE", lambda e: e.matmul(sp_, lhsT=ONES[:], rhs=SQ[:, qk, :], start=True, stop=True), r=["ONES", ("SQ", qk)], w=[sk_])
                        s = nxt("std", 2)
                        if qk == 0:
                            op("ACT", lambda e: e.activation(out=STD[:, s, :], in_=sp_, func=AF.Sqrt, scale=1.0, bias=128.0 * EPS), r=[sk_], w=[("STD", s)])
                        else:
                            op("ACT", lambda e: e.activation(out=STD[:, s, :], in_=sp_, func=AF.Sqrt, scale=1.0 / 128.0, bias=EPS), r=[sk_], w=[("STD", s)])
                        op("DVE", lambda e: e.reciprocal(out=RSTD[:, s, :], in_=STD[:, s, :]), r=[("STD", s)], w=[("RSTD", s)])
                        o4 = nxt("ob", 4)
                        gcol = vo + VOFF["q_norm" if qk == 0 else "k_norm"]
                        op("DVE", lambda e: e.scalar_tensor_tensor(out=OB[:, o4, :], in0=pp, scalar=VEC[:, gcol:gcol + 1], in1=RSTD[:, s, :],
                                                                   op0=ALU.mult, op1=ALU.mult),
                           r=[pk, ("RSTD", s)], w=[("OB", o4)])
                        dst = qT if qk == 0 else kT
                        dma("POOL", lambda e, dst=dst: e.dma_start(out=dst[h, :, tsl], in_=OB[:, o4, :]), r=[("OB", o4)], w=[("D", "qk")])
                for qv in range(4):
                    wk, wb = wnext(2048)
                    for t4 in range(4):
                        pk, pp = bank()
                        mm_group(pk, pp[:, 0:256], [H[:, k, t4 * 128:(t4 + 1) * 128] for k in range(8)],
                                 [wb[:, k * 256:(k + 1) * 256] for k in range(8)], [[wk, ("H", k)] for k in range(8)])
                        v2 = nxt("vo", 2)
                        op("DVE", lambda e: e.tensor_copy(out=VO[:, v2, :], in_=pp[:, 0:256]), r=[pk], w=[("VO", v2)])
                        dma("POOL", lambda e: e.dma_start(out=vv.rearrange("(n f p) c -> n f p c", f=4, p=128)[it, t4, :, qv * 256:(qv + 1) * 256], in_=VO[:, v2, :]),
                            r=[("VO", v2)], w=[("D", "vv")])
                wk, wb = wnext(64)
                fk, fp = bank()
                mm_group(fk, fp[0:8, :], [wb[:, k * 8:(k + 1) * 8] for k in range(8)], [H[:, k, :] for k in range(8)],
                         [[wk, ("H", k)] for k in range(8)])
                op("ACT", lambda e: e.activation(out=FE[:], in_=fp[0:8, :], func=AF.Exp, scale=-1.0, bias=NFB[0:8, l:l + 1]), r=[fk], w=["FE"])
                op("ACT", lambda e: e.activation(out=FE[:], in_=FE[:], func=AF.Ln, bias=1.0), r=["FE"], w=["FE"])
                op("ACT", lambda e: e.mul(out=FE[:], in_=FE[:], mul=-1.0), r=["FE"], w=["FE"])
                op("DVE", lambda e: e.tensor_tensor(out=FE[:, 0:1], in0=FE[:, 0:1], in1=DL[:, 0:1], op=ALU.add), r=["FE", "DL"], w=["FE"])
                op("DVE", lambda e: e.tensor_tensor_scan(out=FD[:], data0=ONE8[:], data1=FE[:], initial=0.0, op0=ALU.mult, op1=ALU.add),
                   r=["FE", "ONE8"], w=["FD"])
                op("DVE", lambda e: e.tensor_copy(out=DL[:, 0:1], in_=FD[:, TT - 1:TT]), r=["FD"], w=["DL"])
                dma("POOL", lambda e: e.dma_start(out=Dd[:, tsl], in_=FD[:]), r=["FD"], w=[("D", "Dd")])
                tk, tp = bank()
                for j4 in range(4):
                    op("PE", lambda e, j4=j4: e.matmul(tp[:, j4 * 8:(j4 + 1) * 8], lhsT=FD[:, j4 * 128:(j4 + 1) * 128], rhs=NEGI[:], start=True, stop=True),
                       r=["FD", "NEGI"], w=[tk])
                op("ACT", lambda e: e.activation(out=NDK[:, tsl_(it, 4), :], in_=tp[:, 0:32].rearrange("p (a b) -> p a b", b=8), func=AF.Copy),
                   r=[tk], w=["NDK"])
                for c in range(8):
                    wk, wb = wnext(2048)
                    for ab in range(2):
                        pk, pp = proj_chunk(wk, wb, ab * 1024)
                        o4 = nxt("ob", 4)
                        mcol = vo + VOFF["merge_b"] + ab * 8 + c
                        op("ACT", lambda e: e.activation(out=OB[:, o4, :], in_=pp, func=AF.Sigmoid, bias=VEC[:, mcol:mcol + 1]), r=[pk], w=[("OB", o4)])
                        row = (ab * 8 + c) * 128
                        dma("POOL", lambda e: e.dma_start(out=sgT[row:row + 128, tsl], in_=OB[:, o4, :]), r=[("OB", o4)], w=[("D", "sgT")])
                wend()
            kb.reset()
            if stop == ("A", l):
                break
            dma("SP", lambda e: e.dma_start(out=MASK, in_=mask_d.rearrange("r p c -> p r c")), w=["MASK"])
            for hh in range(NH):
                dma("SP", lambda e: e.dma_start(out=KTb[:], in_=kT[hh, :, :]), r=[("D", "qk")], w=["KT"])
                dma("SP", lambda e: e.dma_start(out=VHb[:], in_=vv.rearrange("(j p) c -> p j c", p=128)[:, :, tsl_(hh, 128)]), r=[("D", "vv")], w=["VH"])
                op("DVE", lambda e: e.tensor_copy(out=NDKH[:], in_=NDK[:, :, tsl_(hh, 1)].rearrange("p j o -> p (j o)")), r=["NDK"], w=["NDKH"])
                for i in range(NT):
                    q2 = nxt("qt", 2)
                    dma("SP", lambda e: e.dma_start(out=QTb[:, q2, :], in_=qT[hh, :, i * TT:(i + 1) * TT]), r=[("D", "qk")], w=[("QT", q2)])
                    dma("SP", lambda e: e.dma_start(out=DQb[:, q2, :], in_=Dd[tsl_(hh, 1), i * TT:(i + 1) * TT].partition_broadcast(128)),
                        r=[("D", "Dd")], w=[("DQ", q2)])
                    ab_ = nxt("acc", 2)
                    ok_, opp = ("ps", 2 * ab_), PS[:, 2 * ab_, :]
                    lk_, lpp = ("ps", 2 * ab_ + 1), PS[:, 2 * ab_ + 1, :]

                    def key_tile(g_, t_, r, first, last):
                        sb_ = 4 + nxt("sbank", 4)
                        sk_, spp = ("ps", sb_), PS[:, sb_, :]
                        op("PE", lambda e: e.matmul(spp, lhsT=KT4[:, g_, t_, :], rhs=QTb[:, q2, :], start=True, stop=True), r=["KT", ("QT", q2)], w=[sk_])
                        t2 = nxt("tt", 2)
                        op("DVE", lambda e: e.tensor_tensor(out=TTb[:, t2, :], in0=spp, in1=DQb[:, q2, :], op=ALU.add), r=[sk_, ("DQ", q2)], w=[("TT", t2)])
                        if r is not None:
                            op("DVE", lambda e: e.tensor_tensor(out=TTb[:, t2, :], in0=TTb[:, t2, :], in1=MASK[:, r, :], op=ALU.add),
                               r=["MASK"], w=[("TT", t2)])
                        p2 = nxt("pp", 2)
                        op("ACT", lambda e: e.activation(out=PPb[:, p2, :], in_=TTb[:, t2, :], func=AF.Exp, bias=NDKH4[:, g_, t_:t_ + 1]), r=[("TT", t2), "NDKH"], w=[("PP", p2)])
                        op("PE", lambda e: e.matmul(opp, lhsT=VH4[:, g_, t_, :],
                                                    rhs=PPb[:, p2, :], start=first, stop=last), r=["VH", ("PP", p2)], w=[ok_])
                        op("PE", lambda e: e.matmul(lpp, lhsT=ONES[:], rhs=PPb[:, p2, :], start=first, stop=last), r=["ONES", ("PP", p2)], w=[lk_])

                    j0 = 4 * i
                    key_tile(i, 0, 0, True, False)
                    if i > 0:
                        for g in range(i):
                            for t in range(4):
                                key_tile(g, t, None, False, False)
                    for r_ in (1, 2, 3):
                        key_tile(i, r_, r_, False, r_ == 3)
                    r2 = nxt("rl", 2)
                    op("DVE", lambda e: e.reciprocal(out=RLt[:, r2, :], in_=lpp), r=[lk_], w=[("RL", r2)])
                    o4 = nxt("ob", 4)
                    op("DVE", lambda e: e.tensor_tensor(out=OB[:, o4, :], in0=opp, in1=RLt[:, r2, :], op=ALU.mult), r=[ok_, ("RL", r2)], w=[("OB", o4)])
                    dma("POOL", lambda e: e.dma_start(out=atT[tsl_(hh, 128), i * TT:(i + 1) * TT], in_=OB[:, o4, :]), r=[("OB", o4)], w=[("D", "atT")])
            kb.reset()
            if stop == ("B", l):
                break
            for it in range(NT):
                tsl = tsl_(it, TT)
                wbegin(l, N_A, len(BLOCK_E) - N_A)
                dma("SP", lambda e: e.dma_start(out=X[:], in_=xview(xs)[:, :, tsl]), r=[("D", "xs")], w=[("X", c) for c in range(8)])
                dma("SP", lambda e: e.dma_start(out=YRt[:], in_=xview(yrT)[:, :, tsl]), r=[("D", "yrT")], w=["YR"])
                dma("SP", lambda e: e.dma_start(out=ATt[:], in_=xview(atT)[:, :, tsl]), r=[("D", "atT")], w=["AT"])
                dma("SP", lambda e: e.dma_start(out=SGt[:], in_=xview(sgT)[:, :, tsl]), r=[("D", "sgT")], w=["SGT"])
                dma("POOL", lambda e: e.dma_start(out=PTt[:], in_=pin[l].rearrange("(c p) t -> p c t", p=128)[:, :, tsl]), w=["PT"])
                for d in range(8):
                    wk, wb = wnext(2048)
                    ak, ap_ = proj_chunk(wk, wb, 0, rhs=[YRt[:, k, :] for k in range(8)], rkeys=["YR"] * 8)
                    bk, bp_ = proj_chunk(wk, wb, 1024, rhs=[ATt[:, k, :] for k in range(8)], rkeys=["AT"] * 8)
                    m2 = nxt("m", 2)
                    op("DVE", lambda e: e.tensor_tensor(out=M1[:, m2, :], in0=ap_, in1=SGt[:, d, :], op=ALU.mult), r=[ak, "SGT"], w=[("M1", m2)])
                    op("DVE", lambda e: e.tensor_tensor(out=M2[:, m2, :], in0=bp_, in1=SGt[:, 8 + d, :], op=ALU.mult), r=[bk, "SGT"], w=[("M2", m2)])
                    op("DVE", lambda e: e.tensor_tensor(out=MG[:, d, :], in0=M1[:, m2, :], in1=M2[:, m2, :], op=ALU.add), r=[("M1", m2), ("M2", m2)], w=[("MG", d)])
                for dp in range(4):
                    wk, wb = wnext(2048)
                    for s2 in range(2):
                        d = 2 * dp + s2
                        pk, pp = proj_chunk(wk, wb, s2 * 1024, rhs=[MG[:, k, :] for k in range(8)], rkeys=[("MG", k) for k in range(8)])
                        op("DVE", lambda e: e.tensor_tensor(out=X[:, d, :], in0=pp, in1=X[:, d, :], op=ALU.add), r=[pk, ("X", d)], w=[("X", d)])
                if debug:
                    dma("POOL", lambda e: e.dma_start(out=xview(xdbg[0])[:, :, tsl], in_=X[:]), r=[("X", c) for c in range(8)])
                rms(vo + VOFF["ffn2_norm"])
                ffn()
                if debug:
                    dma("POOL", lambda e: e.dma_start(out=xview(xdbg[1])[:, :, tsl], in_=X[:]), r=[("X", c) for c in range(8)])
                rms(vo + VOFF["ple_norm"])
                for dp in range(4):
                    wk, wb = wnext(2048)
                    gm = []
                    for s2 in range(2):
                        d = 2 * dp + s2
                        pk, pp = proj_chunk(wk, wb, s2 * 1024)
                        bcol = vo + VOFF["ple_b_gate"] + d
                        m2 = nxt("m", 2)
                        op("ACT", lambda e: e.activation(out=M1[:, m2, :], in_=pp, func=AF.Sigmoid, bias=VEC[:, bcol:bcol + 1]), r=[pk], w=[("M1", m2)])
                        gm.append(m2)
                    wk2, wb2 = wnext(512)
                    for s2 in range(2):
                        d = 2 * dp + s2
                        m2 = gm[s2]
                        ek, ep = proj_chunk(wk2, wb2, s2 * 256, nk=2, rhs=[PTt[:, k, :] for k in range(2)], rkeys=["PT"] * 2)
                        op("DVE", lambda e: e.tensor_tensor(out=M2[:, m2, :], in0=ep, in1=M1[:, m2, :], op=ALU.mult), r=[ek, ("M1", m2)], w=[("M2", m2)])
                        op("DVE", lambda e: e.tensor_tensor(out=X[:, d, :], in0=M2[:, m2, :], in1=X[:, d, :], op=ALU.add), r=[("M2", m2), ("X", d)], w=[("X", d)])
                wend()
                if debug:
                    dma("POOL", lambda e: e.dma_start(out=xview(xdbg[2])[:, :, tsl], in_=X[:]), r=[("X", c) for c in range(8)])
                if l == L - 1:
                    rms(L * NVL, out_final=True, tsl=tsl)
                else:
                    dma("POOL", lambda e: e.dma_start(out=xview(xs)[:, :, tsl], in_=X[:]), r=[("X", c) for c in range(8)], w=[("D", "xs")])
            kb.reset()
        kb.softbar()
    return nc


_PROG = {}


def get_program(S, L, stop=None):
    key = (S, L, stop)
    if key not in _PROG:
        _PROG[key] = build_program(S, L, stop)
    return _PROG[key]


def kernel(**inputs):
    inp = {k: np.asarray(v) for k, v in inputs.items()}
    B, S, _ = inp["x"].shape
    L = inp["ffn1_w_in"].shape[0]
    nc = get_program(S, L)
    shared = prep_shared(inp, L)
    in_maps = []
    for core in range(8):
        b = core % B
        m = dict(shared)
        m["xT"] = np.ascontiguousarray(inp["x"][b].T)
        m["pT"] = np.ascontiguousarray(inp["p"][:, b].transpose(0, 2, 1))
        in_maps.append(m)
    res = run_bass_kernel_spmd(nc, in_maps, core_ids=list(range(8)))
    out = np.stack([np.ascontiguousarray(res.results[b]["outT"].T) for b in range(B)], axis=0)
    return out.astype(np.float32)
```

```python
import math
from contextlib import ExitStack, contextmanager
import numpy as np
import concourse.bass as bass
import concourse.mybir as mybir
from concourse.bass_utils import run_bass_kernel_spmd

F32 = mybir.dt.float32
BF16 = mybir.dt.bfloat16
AF = mybir.ActivationFunctionType
ALU = mybir.AluOpType

D = 1024
DFF = 2816
NF = DFF // 128
NH = 8
PLE = 256
EPS = 1e-6
TT = 512
CASTW = 8192


VEC_NAMES = [("ffn1_norm", 8), ("mix_norm", 8), ("conv_w", 32), ("conv_b", 8), ("rg_ba", 8), ("rg_bx", 8),
             ("rg_lambda", 8), ("merge_b", 16), ("q_norm", 1), ("k_norm", 1), ("ffn2_norm", 8),
             ("ple_norm", 8), ("ple_b_gate", 8), ("f_b", 1)]
VOFF = {}
_o = 0
for _n, _w in VEC_NAMES:
    VOFF[_n] = _o
    _o += _w
NVL = _o


def pc(v):
    return np.ascontiguousarray(v.reshape(-1, 128).T)


def kxn(w):
    K, n = w.shape
    return np.ascontiguousarray(w.reshape(K // 128, 128, n).transpose(1, 0, 2)).reshape(128, -1)


def ffn_blocks(w_in, w_out):
    out = []
    for f in range(NF):
        g = kxn(w_in[:, f * 128:(f + 1) * 128])
        u = kxn(w_in[:, DFF + f * 128:DFF + (f + 1) * 128])
        out.append(np.concatenate([g, u], axis=1))
    for d in range(8):
        out.append(kxn(w_out[:, d * 128:(d + 1) * 128]))
    return out


def layer_blocks(inp, l):
    blocks = ffn_blocks(inp["ffn1_w_in"][l], inp["ffn1_w_out"][l])
    rg = np.zeros((128, 8, 2, 128), np.float32)
    for c in range(8):
        for s in range(2):
            n = 2 * c + s
            rg[s * 64:(s + 1) * 64, c, 0, s * 64:(s + 1) * 64] = inp["rg_wa"][l, n]
            rg[s * 64:(s + 1) * 64, c, 1, s * 64:(s + 1) * 64] = inp["rg_wx"][l, n]
    blocks.append(rg.reshape(128, -1))
    W = inp["w_in"][l]
    for c in range(8):
        blocks.append(np.concatenate([kxn(W[:, c * 128:(c + 1) * 128]),
                                      kxn(W[:, 1024 + c * 128:1024 + (c + 1) * 128])], axis=1))
    for h in range(8):
        blocks.append(np.concatenate([kxn(W[:, 2048 + h * 128:2048 + (h + 1) * 128]),
                                      kxn(W[:, 3072 + h * 128:3072 + (h + 1) * 128])], axis=1))
    for qv in range(4):
        blocks.append(kxn(W[:, 4096 + qv * 256:4096 + (qv + 1) * 256]))
    blocks.append(kxn(W[:, 5120:5128]))
    for c in range(8):
        blocks.append(np.concatenate([kxn(W[:, 5128 + c * 128:5128 + (c + 1) * 128]),
                                      kxn(W[:, 6152 + c * 128:6152 + (c + 1) * 128])], axis=1))
    for d in range(8):
        blocks.append(np.concatenate([kxn(inp["w_rnn_out"][l][:, d * 128:(d + 1) * 128]),
                                      kxn(inp["w_attn_out"][l][:, d * 128:(d + 1) * 128])], axis=1))
    for dp in range(4):
        blocks.append(np.concatenate([kxn(inp["w_o"][l][:, (2 * dp) * 128:(2 * dp + 1) * 128]),
                                      kxn(inp["w_o"][l][:, (2 * dp + 1) * 128:(2 * dp + 2) * 128])], axis=1))
    blocks += ffn_blocks(inp["ffn2_w_in"][l], inp["ffn2_w_out"][l])
    for dp in range(4):
        blocks.append(np.concatenate([kxn(inp["ple_w_gate"][l][:, (2 * dp) * 128:(2 * dp + 1) * 128]),
                                      kxn(inp["ple_w_gate"][l][:, (2 * dp + 1) * 128:(2 * dp + 2) * 128])], axis=1))
        blocks.append(np.concatenate([kxn(inp["ple_w_proj"][l][:, (2 * dp) * 128:(2 * dp + 1) * 128]),
                                      kxn(inp["ple_w_proj"][l][:, (2 * dp + 1) * 128:(2 * dp + 2) * 128])], axis=1))
    return blocks


BLOCK_E = ([2048] * NF + [2816] * 8 + [2048] + [2048] * 8 + [2048] * 8 + [2048] * 4 + [64] + [2048] * 8
           + [2048] * 8 + [2048] * 4 + [2048] * NF + [2816] * 8 + [2048, 512] * 4)
N_A = NF + 8 + 1 + 8 + 8 + 4 + 1 + 8
EL = sum(BLOCK_E)
ELP = ((EL + CASTW - 1) // CASTW) * CASTW
BLOCK_OFF = np.concatenate([[0], np.cumsum(BLOCK_E)]).astype(int)


def prep_shared(inp, L):
    ws = np.zeros((L, 128, ELP), np.float32)
    vecs = np.zeros((128, L * NVL + 8), np.float32)
    for l in range(L):
        bl = layer_blocks(inp, l)
        assert [b.shape[1] for b in bl] == BLOCK_E
        ws[l, :, :EL] = np.concatenate(bl, axis=1)
        o = l * NVL
        for n, w in VEC_NAMES:
            if n == "conv_w":
                v = np.concatenate([pc(inp["conv_w"][l, k]) for k in range(4)], axis=1)
            elif n == "f_b":
                v = np.zeros((128, 1), np.float32)
                v[:8, 0] = inp["f_b"][l]
            else:
                v = pc(inp[n][l])
            vecs[:, o + VOFF[n]:o + VOFF[n] + w] = v
    vecs[:, L * NVL:L * NVL + 8] = pc(inp["final_norm"])
    mask = np.zeros((4, 128, 512), np.float32)
    for r in range(4):
        p = np.arange(128)[:, None]
        c = np.arange(512)[None, :]
        mask[r] = np.where(128 * r + p <= c, 0.0, -1e30)
    negi = -np.eye(8, dtype=np.float32)
    ones = np.ones((128, 128), np.float32)
    return {"ws": ws, "vecs": vecs, "mask": mask, "negi": negi, "ones": ones}


class KB:
    def __init__(self, nc, es):
        self.nc, self.es = nc, es
        self.eng = {"PE": nc.tensor, "ACT": nc.scalar, "DVE": nc.vector, "POOL": nc.gpsimd, "SP": nc.sync}
        self.sem = {}
        self.count = {}
        self.lastw = {}
        self.readers = {}
        self.waited = {}
        self.nbar = 0
        for n in ("PE", "ACT", "DVE", "POOL"):
            self._counter(n)

    def _counter(self, name):
        if name not in self.sem:
            self.sem[name] = self.es.enter_context(self.nc.semaphore("c%d" % len(self.sem)))
            self.count[name] = 0
        return self.sem[name]

    @staticmethod
    def _isdram(k):
        return isinstance(k, tuple) and k[0] == "D"

    def _deps(self, en, own, r, w):
        need = {}
        for b in r:
            for cn, v in self.lastw.get(b, {}).items():
                need[cn] = max(need.get(cn, 0), v)
        for b in w:
            for cn, v in self.lastw.get(b, {}).items():
                need[cn] = max(need.get(cn, 0), v)
            for cn, v in self.readers.get(b, {}).items():
                need[cn] = max(need.get(cn, 0), v)
        for cn, v in need.items():
            if cn == own and own == "PE":
                continue
            if self.waited.get((en, cn), 0) >= v:
                continue
            self.waited[(en, cn)] = v
            self.eng[en].wait_ge(self.sem[cn], v)

    def _record(self, cn, r, w):
        val = self.count[cn]
        for b in w:
            if self._isdram(b):
                self.lastw.setdefault(b, {})[cn] = val
            else:
                self.lastw[b] = {cn: val}
                self.readers[b] = {}
        for b in r:
            self.readers.setdefault(b, {})[cn] = val

    def op(self, en, fn, r=(), w=()):
        self._deps(en, en, r, w)
        ins = fn(self.eng[en])
        ins.then_inc(self.sem[en], 1)
        self.count[en] += 1
        self._record(en, r, w)
        return ins

    def dma(self, en, fn, r=(), w=()):
        sb_w = [k for k in w if not self._isdram(k)]
        sb_r = [k for k in r if not self._isdram(k)]
        if sb_w:
            cn = "L:" + repr(sb_w[0])
        elif sb_r:
            cn = "S:" + repr(sb_r[0])
        else:
            cn = "misc"
        self._counter(cn)
        self._deps(en, None, r, w)
        ins = fn(self.eng[en])
        ins.then_inc(self.sem[cn], 16)
        self.count[cn] += 16
        self._record(cn, r, w)
        return ins

    def softbar(self):
        for en in self.eng:
            for cn, v in self.count.items():
                if cn == en or v <= 0 or self.waited.get((en, cn), 0) >= v:
                    continue
                self.waited[(en, cn)] = v
                self.eng[en].wait_ge(self.sem[cn], v)

    def reset(self):
        self.softbar()
        for n in ("PE", "ACT", "DVE", "POOL"):
            if self.count[n] > 0:
                self.sem[n] = self.es.enter_context(self.nc.semaphore("c%s_%d" % (n, self.nbar)))
                self.count[n] = 0
        self.nbar += 1
        self.lastw, self.readers = {}, {}
        self.waited = {k: v for k, v in self.waited.items() if k[1] not in ("PE", "ACT", "DVE", "POOL")}


def tsl_(i, n):
    return slice(i * n, (i + 1) * n)


NBW = 4


def build_program(S, L, stop=None, debug=False):
    NT = S // TT
    NKT = S // 128
    nc = bass.Bass("TRN2", target_bir_lowering=False)
    xin = nc.dram_tensor("xT", [D, S], F32, kind="ExternalInput").ap()
    pin = nc.dram_tensor("pT", [L, PLE, S], F32, kind="ExternalInput").ap()
    ws = nc.dram_tensor("ws", [L, 128, ELP], F32, kind="ExternalInput").ap()
    vecs_d = nc.dram_tensor("vecs", [128, L * NVL + 8], F32, kind="ExternalInput").ap()
    mask_d = nc.dram_tensor("mask", [4, 128, 512], F32, kind="ExternalInput").ap()
    negi_d = nc.dram_tensor("negi", [8, 8], F32, kind="ExternalInput").ap()
    ones_d = nc.dram_tensor("ones", [128, 128], F32, kind="ExternalInput").ap()
    outT = nc.dram_tensor("outT", [D, S], F32, kind="ExternalOutput").ap()
    wbf = nc.dram_tensor("wbf", [L, 128, ELP], BF16).ap()
    kw = dict(kind="ExternalOutput") if debug else {}
    xs = nc.dram_tensor("xs", [D, S], F32, **kw).ap()
    yrT = nc.dram_tensor("yrT", [D, S], BF16, **kw).ap()
    atT = nc.dram_tensor("atT", [D, S], BF16, **kw).ap()
    sgT = nc.dram_tensor("sgT", [2 * D, S], BF16, **kw).ap()
    qT = nc.dram_tensor("qT", [NH, 128, S], BF16, **kw).ap()
    kT = nc.dram_tensor("kT", [NH, 128, S], BF16, **kw).ap()
    vv = nc.dram_tensor("vv", [S, D], BF16, **kw).ap()
    Dd = nc.dram_tensor("Dd", [NH, S], F32, **kw).ap()
    xdbg = [nc.dram_tensor("xd%d" % i, [D, S], F32, **kw).ap() for i in range(3)] if debug else None

    es = ExitStack()

    def sb(name, shape, dt):
        return es.enter_context(nc.sbuf_tensor(name, shape, dt))

    with es:
        X = sb("X", [128, 8, TT], F32)
        H = sb("H", [128, 8, TT], BF16)
        AFt = sb("AF", [128, NF, TT], BF16)
        SQ = sb("SQ", [128, 8, TT], BF16)
        WBall = sb("WB", [128, NBW, 2816], BF16)
        WB = [WBall[:, i, :] for i in range(NBW)]
        VEC = sb("VEC", [128, L * NVL + 8], F32)
        CL = sb("CL", [128, L * 8], F32)
        NFB = sb("NFB", [128, L], F32)
        TMPV = sb("TMPV", [128, L * 8], F32)
        NEGI = sb("NEGI", [8, 8], F32)
        ONES = sb("ONES", [128, 128], BF16)
        ONE8 = sb("ONE8", [8, TT], F32)
        RSTD = sb("RSTD", [128, 2, TT], F32)
        STD = sb("STD", [128, 2, TT], F32)
        SG = sb("SG", [128, 2, TT], F32)
        HL = sb("HL", [128, 8], F32)
        DL = sb("DL", [8, 1], F32)
        OB = sb("OB", [128, 4, TT], BF16)
        FE = sb("FE", [8, TT], F32)
        FD = sb("FD", [8, TT], F32)
        NDK = sb("NDK", [128, NKT, 8], F32)
        NDKH = sb("NDKH", [128, NKT], F32)
        NFA = 8 * (TT + 3) + 18 * TT
        FA = sb("FA", [128, NFA], F32)
        RXH = FA[:, 0:8 * (TT + 3)].rearrange("p (c t) -> p c t", t=TT + 3)
        _fo = 8 * (TT + 3)

        def fa2(i):
            return FA[:, _fo + i * 2 * TT:_fo + (i + 1) * 2 * TT].rearrange("p (b t) -> p b t", t=TT)
        T1, XC, RR, II, AA, MM, UU, HS, GL = [fa2(i) for i in range(9)]
        MASK = FA[:, 0:4 * TT].rearrange("p (r t) -> p r t", t=TT)
        M1, M2, OF = fa2(0), fa2(1), fa2(2)
        CB = sb("CB", [128, 42, TT], BF16)
        RGWt = CB[:, 0:4, :].rearrange("p f t -> p (f t)")
        XCB = CB[:, 4:6, :]
        VO = sb("VO", [128, 2, 256], BF16)
        YRt, ATt, SGt, PTt, MG = CB[:, 0:8, :], CB[:, 8:16, :], CB[:, 16:32, :], CB[:, 32:34, :], CB[:, 34:42, :]
        KTb = AFt[:, :, :].rearrange("p f t -> p (f t)")[:, 0:S]
        VHb = WBall[:, :, :].rearrange("p f t -> p (f t)")[:, 0:NKT * 128].rearrange("p (j d) -> p j d", d=128)
        QTb, PPb = H[:, 0:2, :], H[:, 2:6, :]
        KT4 = KTb.rearrange("p (n f d) -> p n f d", f=4, d=128)
        VH4 = VHb.rearrange("p (n f) d -> p n f d", f=4)
        NDKH4 = NDKH[:, :].rearrange("p (n f) -> p n f", f=4)
        DQb, RLt = X[:, 0:2, :], X[:, 4:6, :]
        TTl = [X[:, 2, :], X[:, 3, :], X[:, 6, :], X[:, 7, :]]
        PS = es.enter_context(nc.psum_tensor("PS", [128, 8, TT], F32))
        es.enter_context(nc.Block())
        kb = KB(nc, es)
        op, dma = kb.op, kb.dma

        rot = {}

        def nxt(name, n):
            i = rot.get(name, 0)
            rot[name] = (i + 1) % n
            return i

        def bank():
            b = nxt("ps", 8)
            return ("ps", b), PS[:, b, :]

        ncast_l = ELP // CASTW
        for l in range(L):
            for c in range(ncast_l):
                dma("POOL", lambda e, l=l, c=c: e.dma_start(out=wbf[l, :, c * CASTW:(c + 1) * CASTW],
                                                             in_=ws[l, :, c * CASTW:(c + 1) * CASTW]), w=[("D", "wbf")])
        dma("SP", lambda e: e.dma_start(out=VEC[:], in_=vecs_d[:, :]), w=["VEC"])
        dma("SP", lambda e: e.dma_start(out=NEGI[:], in_=negi_d[:, :]), w=["NEGI"])
        dma("POOL", lambda e: e.dma_start(out=ONES[:], in_=ones_d[:, :]), w=["ONES"])
        op("DVE", lambda e: e.memset(ONE8[:], 1.0), w=["ONE8"])
        op("DVE", lambda e: e.memset(FA[:], 0.0), w=["RXHall"])
        for l in range(L):
            o = l * NVL
            lc = o + VOFF["rg_lambda"]
            op("ACT", lambda e: e.activation(out=TMPV[:, l * 8:(l + 1) * 8], in_=VEC[:, lc:lc + 8], func=AF.Exp, scale=-1.0),
               r=["VEC"], w=["TMPV"])
            op("ACT", lambda e: e.activation(out=TMPV[:, l * 8:(l + 1) * 8], in_=TMPV[:, l * 8:(l + 1) * 8], func=AF.Ln, bias=1.0),
               r=["TMPV"], w=["TMPV"])
            op("ACT", lambda e: e.mul(out=CL[:, l * 8:(l + 1) * 8], in_=TMPV[:, l * 8:(l + 1) * 8], mul=-8.0), r=["TMPV"], w=["CL"])
            fc = o + VOFF["f_b"]
            op("ACT", lambda e: e.mul(out=NFB[:, l:l + 1], in_=VEC[:, fc:fc + 1], mul=-1.0), r=["VEC"], w=["NFB"])
        kb.reset()

        wst = {"seq": [], "issued": 0, "used": 0, "l": 0}

        def wbegin(l, first, count):
            wst.update(seq=list(range(first, first + count)), issued=0, used=0, l=l)

        def wnext(E):
            seq = wst["seq"]
            while wst["issued"] < min(wst["used"] + NBW, len(seq)):
                b = seq[wst["issued"]]
                sl = wst["issued"] % NBW
                Eb = BLOCK_E[b]
                off = int(BLOCK_OFF[b])
                l = wst["l"]
                dma("SP", lambda e, sl=sl, Eb=Eb, off=off, l=l: e.dma_start(out=WB[sl][:, 0:Eb], in_=wbf[l, :, off:off + Eb]),
                    r=[("D", "wbf")], w=[("W", sl)])
                wst["issued"] += 1
            b = seq[wst["used"]]
            assert BLOCK_E[b] == E, (b, BLOCK_E[b], E)
            sl = wst["used"] % NBW
            wst["used"] += 1
            return ("W", sl), WB[sl]

        def wend():
            assert wst["used"] == len(wst["seq"]), (wst["used"], len(wst["seq"]))

        def mm_group(pk, pp, lhs_list, rhs_list, rk):
            n = len(lhs_list)
            for k in range(n):
                ins = op("PE", lambda e, k=k: e.matmul(pp, lhsT=lhs_list[k], rhs=rhs_list[k], start=(k == 0), stop=(k == n - 1)),
                         r=rk[k], w=[pk])
            return ins

        def rms(gcol, out_final=None, tsl=None):
            for c in range(8):
                op("ACT", lambda e, c=c: e.activation(out=SQ[:, c, :], in_=X[:, c, :], func=AF.Square), r=[("X", c)], w=[("SQ", c)])
            qk_, qp = bank()
            mm_group(qk_, qp, [ONES[:]] * 8, [SQ[:, c, :] for c in range(8)], [[("SQ", c), "ONES"] for c in range(8)])
            s = nxt("std", 2)
            op("ACT", lambda e: e.activation(out=STD[:, s, :], in_=qp, func=AF.Sqrt, scale=1.0 / D, bias=EPS), r=[qk_], w=[("STD", s)])
            op("DVE", lambda e: e.reciprocal(out=RSTD[:, s, :], in_=STD[:, s, :]), r=[("STD", s)], w=[("RSTD", s)])
            for c in range(8):
                if out_final is None:
                    op("DVE", lambda e, c=c: e.scalar_tensor_tensor(out=H[:, c, :], in0=X[:, c, :], scalar=VEC[:, gcol + c:gcol + c + 1],
                                                                    in1=RSTD[:, s, :], op0=ALU.mult, op1=ALU.mult),
                       r=[("X", c), ("RSTD", s)], w=[("H", c)])
                else:
                    o2 = nxt("of", 2)
                    op("DVE", lambda e, c=c: e.scalar_tensor_tensor(out=OF[:, o2, :], in0=X[:, c, :], scalar=VEC[:, gcol + c:gcol + c + 1],
                                                                    in1=RSTD[:, s, :], op0=ALU.mult, op1=ALU.mult),
                       r=[("X", c), ("RSTD", s)], w=[("OF", o2)])
                    dma("POOL", lambda e, c=c: e.dma_start(out=outT[c * 128:(c + 1) * 128, tsl], in_=OF[:, o2, :]), r=[("OF", o2)], w=[("D", "outT")])

        def ffn():
            for f in range(NF):
                wk, wb = wnext(2048)
                gk, gp = bank()
                mm_group(gk, gp, [wb[:, k * 128:(k + 1) * 128] for k in range(8)], [H[:, k, :] for k in range(8)],
                         [[wk, ("H", k)] for k in range(8)])
                uk, up = bank()
                mm_group(uk, up, [wb[:, 1024 + k * 128:1024 + (k + 1) * 128] for k in range(8)], [H[:, k, :] for k in range(8)],
                         [[wk, ("H", k)] for k in range(8)])
                s = nxt("sg", 2)
                op("ACT", lambda e: e.activation(out=SG[:, s, :], in_=gp, func=AF.Silu), r=[gk], w=[("SG", s)])
                op("DVE", lambda e: e.tensor_tensor(out=AFt[:, f, :], in0=up, in1=SG[:, s, :], op=ALU.mult), r=[uk, ("SG", s)], w=[("AF", f)])
            for d in range(8):
                wk, wb = wnext(2816)
                yk, yp = bank()
                mm_group(yk, yp, [wb[:, f * 128:(f + 1) * 128] for f in range(NF)], [AFt[:, f, :] for f in range(NF)],
                         [[wk, ("AF", f)] for f in range(NF)])
                op("DVE", lambda e: e.scalar_tensor_tensor(out=X[:, d, :], in0=yp, scalar=0.5, in1=X[:, d, :], op0=ALU.mult, op1=ALU.add),
                   r=[yk, ("X", d)], w=[("X", d)])

        def proj_chunk(wk, wb, woff, nk=8, rhs=None, rkeys=None):
            pk, pp = bank()
            rhs = rhs or [H[:, k, :] for k in range(nk)]
            rkeys = rkeys or [("H", k) for k in range(nk)]
            mm_group(pk, pp, [wb[:, woff + k * 128:woff + (k + 1) * 128] for k in range(nk)], rhs, [[wk, rkeys[k]] for k in range(nk)])
            return pk, pp

        xview = lambda t: t.rearrange("(c p) t -> p c t", p=128)

        for l in range(L):
            vo = l * NVL
            xsrc = xin if l == 0 else xs
            op("DVE", lambda e: e.memset(HL[:], 0.0), w=["HL"])
            op("DVE", lambda e: e.memset(DL[:], 0.0), w=["DL"])
            op("DVE", lambda e: e.memset(RXH[:, :, 0:3], 0.0), w=["RXHall"])
            for it in range(NT):
                tsl = tsl_(it, TT)
                wbegin(l, 0, N_A)
                dma("SP", lambda e: e.dma_start(out=X[:], in_=xview(xsrc)[:, :, tsl]), r=[("D", "xs")], w=[("X", c) for c in range(8)])
                rms(vo + VOFF["ffn1_norm"])
                ffn()
                dma("POOL", lambda e: e.dma_start(out=xview(xs)[:, :, tsl], in_=X[:]), r=[("X", c) for c in range(8)], w=[("D", "xs")])
                rms(vo + VOFF["mix_norm"])
                wk, wb = wnext(2048)
                op("DVE", lambda e: e.tensor_copy(out=RGWt[:], in_=wb[:, 0:2048]), r=[wk], w=["RGW"])
                cw = vo + VOFF["conv_w"]
                cbc = vo + VOFF["conv_b"]
                for c in range(8):
                    b2 = c % 2
                    wk, wb = wnext(2048)
                    rxk, rxp = proj_chunk(wk, wb, 0)
                    gtk, gtp = proj_chunk(wk, wb, 1024)
                    op("ACT", lambda e: e.activation(out=RXH[:, c, 3:TT + 3], in_=rxp, func=AF.Copy), r=[rxk], w=[("RXH", c)])
                    op("DVE", lambda e: e.tensor_scalar(out=T1[:, b2, :], in0=RXH[:, c, 0:TT], scalar1=VEC[:, cw + c:cw + c + 1],
                                                        scalar2=VEC[:, cbc + c:cbc + c + 1], op0=ALU.mult, op1=ALU.add),
                       r=[("RXH", c)], w=[("T1", b2)])
                    for kk in (1, 2):
                        op("DVE", lambda e, kk=kk: e.scalar_tensor_tensor(out=T1[:, b2, :], in0=RXH[:, c, kk:TT + kk],
                                                                          scalar=VEC[:, cw + 8 * kk + c:cw + 8 * kk + c + 1],
                                                                          in1=T1[:, b2, :], op0=ALU.mult, op1=ALU.add),
                           r=[("RXH", c)], w=[("T1", b2)])
                    op("DVE", lambda e: e.scalar_tensor_tensor(out=XC[:, b2, :], in0=RXH[:, c, 3:TT + 3], scalar=VEC[:, cw + 24 + c:cw + 25 + c],
                                                               in1=T1[:, b2, :], op0=ALU.mult, op1=ALU.add),
                       r=[("RXH", c), ("T1", b2)], w=[("XC", b2)])
                    op("DVE", lambda e: e.tensor_copy(out=RXH[:, c, 0:3], in_=RXH[:, c, TT:TT + 3]), r=[("RXH", c)], w=[("RXH", c)])
                    op("DVE", lambda e: e.tensor_copy(out=XCB[:, b2, :], in_=XC[:, b2, :]), r=[("XC", b2)], w=[("XCB", b2)])
                    rk_, rp = bank()
                    op("PE", lambda e: e.matmul(rp, lhsT=RGWt[:, c * 256:c * 256 + 128], rhs=XCB[:, b2, :], start=True, stop=True),
                       r=["RGW", ("XCB", b2)], w=[rk_])
                    ik_, ip = bank()
                    op("PE", lambda e: e.matmul(ip, lhsT=RGWt[:, c * 256 + 128:c * 256 + 256], rhs=XCB[:, b2, :], start=True, stop=True),
                       r=["RGW", ("XCB", b2)], w=[ik_])
                    bac = vo + VOFF["rg_ba"] + c
                    bxc = vo + VOFF["rg_bx"] + c
                    op("ACT", lambda e: e.activation(out=RR[:, b2, :], in_=rp, func=AF.Sigmoid, bias=VEC[:, bac:bac + 1]), r=[rk_], w=[("RR", b2)])
                    op("ACT", lambda e: e.activation(out=AA[:, b2, :], in_=RR[:, b2, :], func=AF.Exp, scale=CL[:, l * 8 + c:l * 8 + c + 1]),
                       r=[("RR", b2)], w=[("AA", b2)])
                    op("ACT", lambda e: e.activation(out=II[:, b2, :], in_=ip, func=AF.Sigmoid, bias=VEC[:, bxc:bxc + 1]), r=[ik_], w=[("II", b2)])
                    op("ACT", lambda e: e.activation(out=MM[:, b2, :], in_=AA[:, b2, :], func=AF.Square), r=[("AA", b2)], w=[("MM", b2)])
                    op("ACT", lambda e: e.activation(out=MM[:, b2, :], in_=MM[:, b2, :], func=AF.Sqrt, scale=-1.0, bias=1.0), r=[("MM", b2)], w=[("MM", b2)])
                    op("ACT", lambda e: e.activation(out=GL[:, b2, :], in_=gtp, func=AF.Gelu_apprx_tanh), r=[gtk], w=[("GL", b2)])
                    op("DVE", lambda e: e.tensor_tensor(out=UU[:, b2, :], in0=II[:, b2, :], in1=XC[:, b2, :], op=ALU.mult),
                       r=[("II", b2), ("XC", b2)], w=[("UU", b2)])
                    op("DVE", lambda e: e.tensor_tensor(out=UU[:, b2, :], in0=UU[:, b2, :], in1=MM[:, b2, :], op=ALU.mult),
                       r=[("MM", b2)], w=[("UU", b2)])
                    op("DVE", lambda e: e.scalar_tensor_tensor(out=UU[:, b2, 0:1], in0=AA[:, b2, 0:1], scalar=HL[:, c:c + 1], in1=UU[:, b2, 0:1],
                                                               op0=ALU.mult, op1=ALU.add),
                       r=[("AA", b2), "HL"], w=[("UU", b2)])
                    op("DVE", lambda e: e.tensor_tensor_scan(out=HS[:, b2, :], data0=AA[:, b2, :], data1=UU[:, b2, :], initial=0.0,
                                                             op0=ALU.mult, op1=ALU.add),
                       r=[("AA", b2), ("UU", b2)], w=[("HS", b2)])
                    op("DVE", lambda e: e.tensor_copy(out=HL[:, c:c + 1], in_=HS[:, b2, TT - 1:TT]), r=[("HS", b2)], w=["HL"])
                    o4 = nxt("ob", 4)
                    op("DVE", lambda e: e.tensor_tensor(out=OB[:, o4, :], in0=HS[:, b2, :], in1=GL[:, b2, :], op=ALU.mult),
                       r=[("HS", b2), ("GL", b2)], w=[("OB", o4)])
                    dma("POOL", lambda e: e.dma_start(out=yrT[c * 128:(c + 1) * 128, tsl], in_=OB[:, o4, :]), r=[("OB", o4)], w=[("D", "yrT")])
                for h in range(NH):
                    wk, wb = wnext(2048)
                    for qk in range(2):
                        pk, pp = proj_chunk(wk, wb, qk * 1024)
                        op("ACT", lambda e: e.activation(out=SQ[:, qk, :], in_=pp, func=AF.Square), r=[pk], w=[("SQ", qk)])
                        sk_, sp_ = bank()
                        op("PE", lambda e: e.matmul(sp_, lhsT=ONES[:], rhs=SQ[:, qk, :], start=True, stop=True), r=["ONES", ("SQ", qk)], w=[sk_])
                        s = nxt("std", 2)
                        if qk == 0:
                            op("ACT", lambda e: e.activation(out=STD[:, s, :], in_=sp_, func=AF.Sqrt, scale=1.0, bias=128.0 * EPS), r=[sk_], w=[("STD", s)])
                        else:
                            op("ACT", lambda e: e.activation(out=STD[:, s, :], in_=sp_, func=AF.Sqrt, scale=1.0 / 128.0, bias=EPS), r=[sk_], w=[("STD", s)])
                        op("DVE", lambda e: e.reciprocal(out=RSTD[:, s, :], in_=STD[:, s, :]), r=[("STD", s)], w=[("RSTD", s)])
                        o4 = nxt("ob", 4)
                        gcol = vo + VOFF["q_norm" if qk == 0 else "k_norm"]
                        op("DVE", lambda e: e.scalar_tensor_tensor(out=OB[:, o4, :], in0=pp, scalar=VEC[:, gcol:gcol + 1], in1=RSTD[:, s, :],
                                                                   op0=ALU.mult, op1=ALU.mult),
                           r=[pk, ("RSTD", s)], w=[("OB", o4)])
                        dst = qT if qk == 0 else kT
                        dma("POOL", lambda e, dst=dst: e.dma_start(out=dst[h, :, tsl], in_=OB[:, o4, :]), r=[("OB", o4)], w=[("D", "qk")])
                for qv in range(4):
                    wk, wb = wnext(2048)
                    for t4 in range(4):
                        pk, pp = bank()
                        mm_group(pk, pp[:, 0:256], [H[:, k, t4 * 128:(t4 + 1) * 128] for k in range(8)],
                                 [wb[:, k * 256:(k + 1) * 256] for k in range(8)], [[wk, ("H", k)] for k in range(8)])
                        v2 = nxt("vo", 2)
                        op("DVE", lambda e: e.tensor_copy(out=VO[:, v2, :], in_=pp[:, 0:256]), r=[pk], w=[("VO", v2)])
                        dma("POOL", lambda e: e.dma_start(out=vv.rearrange("(n f p) c -> n f p c", f=4, p=128)[it, t4, :, qv * 256:(qv + 1) * 256], in_=VO[:, v2, :]),
                            r=[("VO", v2)], w=[("D", "vv")])
                wk, wb = wnext(64)
                fk, fp = bank()
                mm_group(fk, fp[0:8, :], [wb[:, k * 8:(k + 1) * 8] for k in range(8)], [H[:, k, :] for k in range(8)],
                         [[wk, ("H", k)] for k in range(8)])
                op("ACT", lambda e: e.activation(out=FE[:], in_=fp[0:8, :], func=AF.Exp, scale=-1.0, bias=NFB[0:8, l:l + 1]), r=[fk], w=["FE"])
                op("ACT", lambda e: e.activation(out=FE[:], in_=FE[:], func=AF.Ln, bias=1.0), r=["FE"], w=["FE"])
                op("ACT", lambda e: e.mul(out=FE[:], in_=FE[:], mul=-1.0), r=["FE"], w=["FE"])
                op("DVE", lambda e: e.tensor_tensor(out=FE[:, 0:1], in0=FE[:, 0:1], in1=DL[:, 0:1], op=ALU.add), r=["FE", "DL"], w=["FE"])
                op("DVE", lambda e: e.tensor_tensor_scan(out=FD[:], data0=ONE8[:], data1=FE[:], initial=0.0, op0=ALU.mult, op1=ALU.add),
                   r=["FE", "ONE8"], w=["FD"])
                op("DVE", lambda e: e.tensor_copy(out=DL[:, 0:1], in_=FD[:, TT - 1:TT]), r=["FD"], w=["DL"])
                dma("POOL", lambda e: e.dma_start(out=Dd[:, tsl], in_=FD[:]), r=["FD"], w=[("D", "Dd")])
                tk, tp = bank()
                for j4 in range(4):
                    op("PE", lambda e, j4=j4: e.matmul(tp[:, j4 * 8:(j4 + 1) * 8], lhsT=FD[:, j4 * 128:(j4 + 1) * 128], rhs=NEGI[:], start=True, stop=True),
                       r=["FD", "NEGI"], w=[tk])
                op("ACT", lambda e: e.activation(out=NDK[:, tsl_(it, 4), :], in_=tp[:, 0:32].rearrange("p (a b) -> p a b", b=8), func=AF.Copy),
                   r=[tk], w=["NDK"])
                for c in range(8):
                    wk, wb = wnext(2048)
                    for ab in range(2):
                        pk, pp = proj_chunk(wk, wb, ab * 1024)
                        o4 = nxt("ob", 4)
                        mcol = vo + VOFF["merge_b"] + ab * 8 + c
                        op("ACT", lambda e: e.activation(out=OB[:, o4, :], in_=pp, func=AF.Sigmoid, bias=VEC[:, mcol:mcol + 1]), r=[pk], w=[("OB", o4)])
                        row = (ab * 8 + c) * 128
                        dma("POOL", lambda e: e.dma_start(out=sgT[row:row + 128, tsl], in_=OB[:, o4, :]), r=[("OB", o4)], w=[("D", "sgT")])
                wend()
            kb.reset()
            if stop == ("A", l):
                break
            dma("SP", lambda e: e.dma_start(out=MASK, in_=mask_d.rearrange("r p c -> p r c")), w=["MASK"])
            for hh in range(NH):
                dma("SP", lambda e: e.dma_start(out=KTb[:], in_=kT[hh, :, :]), r=[("D", "qk")], w=["KT"])
                dma("SP", lambda e: e.dma_start(out=VHb[:], in_=vv.rearrange("(j p) c -> p j c", p=128)[:, :, tsl_(hh, 128)]), r=[("D", "vv")], w=["VH"])
                op("DVE", lambda e: e.tensor_copy(out=NDKH[:], in_=NDK[:, :, tsl_(hh, 1)].rearrange("p j o -> p (j o)")), r=["NDK"], w=["NDKH"])
                for i in range(NT):
                    q2 = nxt("qt", 2)
                    dma("SP", lambda e: e.dma_start(out=QTb[:, q2, :], in_=qT[hh, :, i * TT:(i + 1) * TT]), r=[("D", "qk")], w=[("QT", q2)])
                    dma("SP", lambda e: e.dma_start(out=DQb[:, q2, :], in_=Dd[tsl_(hh, 1), i * TT:(i + 1) * TT].partition_broadcast(128)),
                        r=[("D", "Dd")], w=[("DQ", q2)])
                    ab_ = nxt("acc", 2)
                    ok_, opp = ("ps", 2 * ab_), PS[:, 2 * ab_, :]
                    lk_, lpp = ("ps", 2 * ab_ + 1), PS[:, 2 * ab_ + 1, :]

                    def qk_phase(g_, t_, r):
                        sb_ = 4 + nxt("sbank", 4)
                        sk_, spp = ("ps", sb_), PS[:, sb_, :]
                        op("PE", lambda e: e.matmul(spp, lhsT=KT4[:, g_, t_, :], rhs=QTb[:, q2, :], start=True, stop=True), r=["KT", ("QT", q2)], w=[sk_])
                        t2 = nxt("tt", 4)
                        op("DVE", lambda e: e.tensor_tensor(out=TTl[t2], in0=spp, in1=DQb[:, q2, :], op=ALU.add), r=[sk_, ("DQ", q2)], w=[("TT", t2)])
                        if r is not None:
                            op("DVE", lambda e: e.tensor_tensor(out=TTl[t2], in0=TTl[t2], in1=MASK[:, r, :], op=ALU.add),
                               r=["MASK", ("TT", t2)], w=[("TT", t2)])
                        p2 = nxt("pp", 4)
                        op("ACT", lambda e: e.activation(out=PPb[:, p2, :], in_=TTl[t2], func=AF.Exp, bias=NDKH4[:, g_, t_:t_ + 1]), r=[("TT", t2), "NDKH"], w=[("PP", p2)])
                        return p2

                    def pv_phase(g_, t_, p2, first, last):
                        op("PE", lambda e: e.matmul(opp, lhsT=VH4[:, g_, t_, :],
                                                    rhs=PPb[:, p2, :], start=first, stop=last), r=["VH", ("PP", p2)], w=[ok_])
                        op("PE", lambda e: e.matmul(lpp, lhsT=ONES[:], rhs=PPb[:, p2, :], start=first, stop=last), r=["ONES", ("PP", p2)], w=[lk_])

                    tiles = [(i, 0, 0)] + [(g, t, None) for g in range(i) for t in range(4)] + [(i, r_, r_) for r_ in (1, 2, 3)]
                    LOOK = 2
                    pbuf = {}
                    for k in range(min(LOOK, len(tiles))):
                        pbuf[k] = qk_phase(*tiles[k])
                    for k in range(len(tiles)):
                        if k + LOOK < len(tiles):
                            pbuf[k + LOOK] = qk_phase(*tiles[k + LOOK])
                        pv_phase(tiles[k][0], tiles[k][1], pbuf.pop(k), k == 0, k == len(tiles) - 1)
                    r2 = nxt("rl", 2)
                    op("DVE", lambda e: e.reciprocal(out=RLt[:, r2, :], in_=lpp), r=[lk_], w=[("RL", r2)])
                    o4 = nxt("ob", 4)
                    op("DVE", lambda e: e.tensor_tensor(out=OB[:, o4, :], in0=opp, in1=RLt[:, r2, :], op=ALU.mult), r=[ok_, ("RL", r2)], w=[("OB", o4)])
                    dma("POOL", lambda e: e.dma_start(out=atT[tsl_(hh, 128), i * TT:(i + 1) * TT], in_=OB[:, o4, :]), r=[("OB", o4)], w=[("D", "atT")])
            kb.reset()
            if stop == ("B", l):
                break
            for it in range(NT):
                tsl = tsl_(it, TT)
                wbegin(l, N_A, len(BLOCK_E) - N_A)
                dma("SP", lambda e: e.dma_start(out=X[:], in_=xview(xs)[:, :, tsl]), r=[("D", "xs")], w=[("X", c) for c in range(8)])
                dma("SP", lambda e: e.dma_start(out=YRt[:], in_=xview(yrT)[:, :, tsl]), r=[("D", "yrT")], w=["YR"])
                dma("SP", lambda e: e.dma_start(out=ATt[:], in_=xview(atT)[:, :, tsl]), r=[("D", "atT")], w=["AT"])
                dma("SP", lambda e: e.dma_start(out=SGt[:], in_=xview(sgT)[:, :, tsl]), r=[("D", "sgT")], w=["SGT"])
                dma("POOL", lambda e: e.dma_start(out=PTt[:], in_=pin[l].rearrange("(c p) t -> p c t", p=128)[:, :, tsl]), w=["PT"])
                for d in range(8):
                    wk, wb = wnext(2048)
                    ak, ap_ = proj_chunk(wk, wb, 0, rhs=[YRt[:, k, :] for k in range(8)], rkeys=["YR"] * 8)
                    bk, bp_ = proj_chunk(wk, wb, 1024, rhs=[ATt[:, k, :] for k in range(8)], rkeys=["AT"] * 8)
                    m2 = nxt("m", 2)
                    op("DVE", lambda e: e.tensor_tensor(out=M1[:, m2, :], in0=ap_, in1=SGt[:, d, :], op=ALU.mult), r=[ak, "SGT"], w=[("M1", m2)])
                    op("DVE", lambda e: e.tensor_tensor(out=M2[:, m2, :], in0=bp_, in1=SGt[:, 8 + d, :], op=ALU.mult), r=[bk, "SGT"], w=[("M2", m2)])
                    op("DVE", lambda e: e.tensor_tensor(out=MG[:, d, :], in0=M1[:, m2, :], in1=M2[:, m2, :], op=ALU.add), r=[("M1", m2), ("M2", m2)], w=[("MG", d)])
                for dp in range(4):
                    wk, wb = wnext(2048)
                    for s2 in range(2):
                        d = 2 * dp + s2
                        pk, pp = proj_chunk(wk, wb, s2 * 1024, rhs=[MG[:, k, :] for k in range(8)], rkeys=[("MG", k) for k in range(8)])
                        op("DVE", lambda e: e.tensor_tensor(out=X[:, d, :], in0=pp, in1=X[:, d, :], op=ALU.add), r=[pk, ("X", d)], w=[("X", d)])
                if debug:
                    dma("POOL", lambda e: e.dma_start(out=xview(xdbg[0])[:, :, tsl], in_=X[:]), r=[("X", c) for c in range(8)])
                rms(vo + VOFF["ffn2_norm"])
                ffn()
                if debug:
                    dma("POOL", lambda e: e.dma_start(out=xview(xdbg[1])[:, :, tsl], in_=X[:]), r=[("X", c) for c in range(8)])
                rms(vo + VOFF["ple_norm"])
                for dp in range(4):
                    wk, wb = wnext(2048)
                    gm = []
                    for s2 in range(2):
                        d = 2 * dp + s2
                        pk, pp = proj_chunk(wk, wb, s2 * 1024)
                        bcol = vo + VOFF["ple_b_gate"] + d
                        m2 = nxt("m", 2)
                        op("ACT", lambda e: e.activation(out=M1[:, m2, :], in_=pp, func=AF.Sigmoid, bias=VEC[:, bcol:bcol + 1]), r=[pk], w=[("M1", m2)])
                        gm.append(m2)
                    wk2, wb2 = wnext(512)
                    for s2 in range(2):
                        d = 2 * dp + s2
                        m2 = gm[s2]
                        ek, ep = proj_chunk(wk2, wb2, s2 * 256, nk=2, rhs=[PTt[:, k, :] for k in range(2)], rkeys=["PT"] * 2)
                        op("DVE", lambda e: e.tensor_tensor(out=M2[:, m2, :], in0=ep, in1=M1[:, m2, :], op=ALU.mult), r=[ek, ("M1", m2)], w=[("M2", m2)])
                        op("DVE", lambda e: e.tensor_tensor(out=X[:, d, :], in0=M2[:, m2, :], in1=X[:, d, :], op=ALU.add), r=[("M2", m2), ("X", d)], w=[("X", d)])
                wend()
                if debug:
                    dma("POOL", lambda e: e.dma_start(out=xview(xdbg[2])[:, :, tsl], in_=X[:]), r=[("X", c) for c in range(8)])
                if l == L - 1:
                    rms(L * NVL, out_final=True, tsl=tsl)
                else:
                    dma("POOL", lambda e: e.dma_start(out=xview(xs)[:, :, tsl], in_=X[:]), r=[("X", c) for c in range(8)], w=[("D", "xs")])
            kb.reset()
        kb.softbar()
    return nc


_PROG = {}


def get_program(S, L, stop=None):
    key = (S, L, stop)
    if key not in _PROG:
        _PROG[key] = build_program(S, L, stop)
    return _PROG[key]


def kernel(**inputs):
    inp = {k: np.asarray(v) for k, v in inputs.items()}
    B, S, _ = inp["x"].shape
    L = inp["ffn1_w_in"].shape[0]
    nc = get_program(S, L)
    shared = prep_shared(inp, L)
    in_maps = []
    for core in range(8):
        b = core % B
        m = dict(shared)
        m["xT"] = np.ascontiguousarray(inp["x"][b].T)
        m["pT"] = np.ascontiguousarray(inp["p"][:, b].transpose(0, 2, 1))
        in_maps.append(m)
    res = run_bass_kernel_spmd(nc, in_maps, core_ids=list(range(8)))
    out = np.stack([np.ascontiguousarray(res.results[b]["outT"].T) for b in range(B)], axis=0)
    return out.astype(np.float32)
```

```python
import math
from contextlib import ExitStack, contextmanager
import numpy as np
import concourse.bass as bass
import concourse.mybir as mybir
from concourse.bass_utils import run_bass_kernel_spmd

F32 = mybir.dt.float32
BF16 = mybir.dt.bfloat16
AF = mybir.ActivationFunctionType
ALU = mybir.AluOpType

D = 1024
DFF = 2816
NF = DFF // 128
NH = 8
PLE = 256
EPS = 1e-6
TT = 512
CASTW = 8192


VEC_NAMES = [("ffn1_norm", 8), ("mix_norm", 8), ("conv_w", 32), ("conv_b", 8), ("rg_ba", 8), ("rg_bx", 8),
             ("rg_lambda", 8), ("merge_b", 16), ("q_norm", 1), ("k_norm", 1), ("ffn2_norm", 8),
             ("ple_norm", 8), ("ple_b_gate", 8), ("f_b", 1)]
VOFF = {}
_o = 0
for _n, _w in VEC_NAMES:
    VOFF[_n] = _o
    _o += _w
NVL = _o


def pc(v):
    return np.ascontiguousarray(v.reshape(-1, 128).T)


def kxn(w):
    K, n = w.shape
    return np.ascontiguousarray(w.reshape(K // 128, 128, n).transpose(1, 0, 2)).reshape(128, -1)


def ffn_blocks(w_in, w_out):
    out = []
    for f in range(NF):
        g = kxn(w_in[:, f * 128:(f + 1) * 128])
        u = kxn(w_in[:, DFF + f * 128:DFF + (f + 1) * 128])
        out.append(np.concatenate([g, u], axis=1))
    for d in range(8):
        out.append(kxn(w_out[:, d * 128:(d + 1) * 128]))
    return out


def layer_blocks(inp, l):
    blocks = ffn_blocks(inp["ffn1_w_in"][l], inp["ffn1_w_out"][l])
    rg = np.zeros((128, 8, 2, 128), np.float32)
    for c in range(8):
        for s in range(2):
            n = 2 * c + s
            rg[s * 64:(s + 1) * 64, c, 0, s * 64:(s + 1) * 64] = inp["rg_wa"][l, n]
            rg[s * 64:(s + 1) * 64, c, 1, s * 64:(s + 1) * 64] = inp["rg_wx"][l, n]
    blocks.append(rg.reshape(128, -1))
    W = inp["w_in"][l]
    for c in range(8):
        blocks.append(np.concatenate([kxn(W[:, c * 128:(c + 1) * 128]),
                                      kxn(W[:, 1024 + c * 128:1024 + (c + 1) * 128])], axis=1))
    for h in range(8):
        blocks.append(np.concatenate([kxn(W[:, 2048 + h * 128:2048 + (h + 1) * 128]),
                                      kxn(W[:, 3072 + h * 128:3072 + (h + 1) * 128])], axis=1))
    for qv in range(4):
        blocks.append(kxn(W[:, 4096 + qv * 256:4096 + (qv + 1) * 256]))
    blocks.append(kxn(W[:, 5120:5128]))
    for c in range(8):
        blocks.append(np.concatenate([kxn(W[:, 5128 + c * 128:5128 + (c + 1) * 128]),
                                      kxn(W[:, 6152 + c * 128:6152 + (c + 1) * 128])], axis=1))
    for d in range(8):
        blocks.append(np.concatenate([kxn(inp["w_rnn_out"][l][:, d * 128:(d + 1) * 128]),
                                      kxn(inp["w_attn_out"][l][:, d * 128:(d + 1) * 128])], axis=1))
    for dp in range(4):
        blocks.append(np.concatenate([kxn(inp["w_o"][l][:, (2 * dp) * 128:(2 * dp + 1) * 128]),
                                      kxn(inp["w_o"][l][:, (2 * dp + 1) * 128:(2 * dp + 2) * 128])], axis=1))
    blocks += ffn_blocks(inp["ffn2_w_in"][l], inp["ffn2_w_out"][l])
    for dp in range(4):
        blocks.append(np.concatenate([kxn(inp["ple_w_gate"][l][:, (2 * dp) * 128:(2 * dp + 1) * 128]),
                                      kxn(inp["ple_w_gate"][l][:, (2 * dp + 1) * 128:(2 * dp + 2) * 128])], axis=1))
        blocks.append(np.concatenate([kxn(inp["ple_w_proj"][l][:, (2 * dp) * 128:(2 * dp + 1) * 128]),
                                      kxn(inp["ple_w_proj"][l][:, (2 * dp + 1) * 128:(2 * dp + 2) * 128])], axis=1))
    return blocks


BLOCK_E = ([2048] * NF + [2816] * 8 + [2048] + [2048] * 8 + [2048] * 8 + [2048] * 4 + [64] + [2048] * 8
           + [2048] * 8 + [2048] * 4 + [2048] * NF + [2816] * 8 + [2048, 512] * 4)
N_A = NF + 8 + 1 + 8 + 8 + 4 + 1 + 8
EL = sum(BLOCK_E)
ELP = ((EL + CASTW - 1) // CASTW) * CASTW
BLOCK_OFF = np.concatenate([[0], np.cumsum(BLOCK_E)]).astype(int)


def prep_shared(inp, L):
    ws = np.zeros((L, 128, ELP), np.float32)
    vecs = np.zeros((128, L * NVL + 8), np.float32)
    for l in range(L):
        bl = layer_blocks(inp, l)
        assert [b.shape[1] for b in bl] == BLOCK_E
        ws[l, :, :EL] = np.concatenate(bl, axis=1)
        o = l * NVL
        for n, w in VEC_NAMES:
            if n == "conv_w":
                v = np.concatenate([pc(inp["conv_w"][l, k]) for k in range(4)], axis=1)
            elif n == "f_b":
                v = np.zeros((128, 1), np.float32)
                v[:8, 0] = inp["f_b"][l]
            else:
                v = pc(inp[n][l])
            vecs[:, o + VOFF[n]:o + VOFF[n] + w] = v
    vecs[:, L * NVL:L * NVL + 8] = pc(inp["final_norm"])
    mask = np.zeros((4, 128, 512), np.float32)
    for r in range(4):
        p = np.arange(128)[:, None]
        c = np.arange(512)[None, :]
        mask[r] = np.where(128 * r + p <= c, 0.0, -1e30)
    negi = -np.eye(8, dtype=np.float32)
    ones = np.ones((128, 128), np.float32)
    return {"ws": ws, "vecs": vecs, "mask": mask, "negi": negi, "ones": ones}


class KB:
    def __init__(self, nc, es):
        self.nc, self.es = nc, es
        self.eng = {"PE": nc.tensor, "ACT": nc.scalar, "DVE": nc.vector, "POOL": nc.gpsimd, "SP": nc.sync}
        self.sem = {}
        self.count = {}
        self.lastw = {}
        self.readers = {}
        self.waited = {}
        self.nbar = 0
        for n in ("PE", "ACT", "DVE", "POOL"):
            self._counter(n)

    def _counter(self, name):
        if name not in self.sem:
            self.sem[name] = self.es.enter_context(self.nc.semaphore("c%d" % len(self.sem)))
            self.count[name] = 0
        return self.sem[name]

    @staticmethod
    def _isdram(k):
        return isinstance(k, tuple) and k[0] == "D"

    def _deps(self, en, own, r, w):
        need = {}
        for b in r:
            for cn, v in self.lastw.get(b, {}).items():
                need[cn] = max(need.get(cn, 0), v)
        for b in w:
            for cn, v in self.lastw.get(b, {}).items():
                need[cn] = max(need.get(cn, 0), v)
            for cn, v in self.readers.get(b, {}).items():
                need[cn] = max(need.get(cn, 0), v)
        for cn, v in need.items():
            if cn == own and own == "PE":
                continue
            if self.waited.get((en, cn), 0) >= v:
                continue
            self.waited[(en, cn)] = v
            self.eng[en].wait_ge(self.sem[cn], v)

    def _record(self, cn, r, w):
        val = self.count[cn]
        for b in w:
            if self._isdram(b):
                self.lastw.setdefault(b, {})[cn] = val
            else:
                self.lastw[b] = {cn: val}
                self.readers[b] = {}
        for b in r:
            self.readers.setdefault(b, {})[cn] = val

    def op(self, en, fn, r=(), w=()):
        self._deps(en, en, r, w)
        ins = fn(self.eng[en])
        ins.then_inc(self.sem[en], 1)
        self.count[en] += 1
        self._record(en, r, w)
        return ins

    def dma(self, en, fn, r=(), w=()):
        sb_w = [k for k in w if not self._isdram(k)]
        sb_r = [k for k in r if not self._isdram(k)]
        if sb_w:
            cn = "L:" + repr(sb_w[0])
        elif sb_r:
            cn = "S:" + repr(sb_r[0])
        else:
            cn = "misc"
        self._counter(cn)
        self._deps(en, None, r, w)
        ins = fn(self.eng[en])
        ins.then_inc(self.sem[cn], 16)
        self.count[cn] += 16
        self._record(cn, r, w)
        return ins

    def softbar(self):
        for en in self.eng:
            for cn, v in self.count.items():
                if cn == en or v <= 0 or self.waited.get((en, cn), 0) >= v:
                    continue
                self.waited[(en, cn)] = v
                self.eng[en].wait_ge(self.sem[cn], v)

    def reset(self):
        self.softbar()
        for n in ("PE", "ACT", "DVE", "POOL"):
            if self.count[n] > 0:
                self.sem[n] = self.es.enter_context(self.nc.semaphore("c%s_%d" % (n, self.nbar)))
                self.count[n] = 0
        self.nbar += 1
        self.lastw, self.readers = {}, {}
        self.waited = {k: v for k, v in self.waited.items() if k[1] not in ("PE", "ACT", "DVE", "POOL")}


def tsl_(i, n):
    return slice(i * n, (i + 1) * n)


NBW = 4


def build_program(S, L, stop=None, debug=False):
    NT = S // TT
    NKT = S // 128
    nc = bass.Bass("TRN2", target_bir_lowering=False)
    xin = nc.dram_tensor("xT", [D, S], F32, kind="ExternalInput").ap()
    pin = nc.dram_tensor("pT", [L, PLE, S], F32, kind="ExternalInput").ap()
    ws = nc.dram_tensor("ws", [L, 128, ELP], F32, kind="ExternalInput").ap()
    vecs_d = nc.dram_tensor("vecs", [128, L * NVL + 8], F32, kind="ExternalInput").ap()
    mask_d = nc.dram_tensor("mask", [4, 128, 512], F32, kind="ExternalInput").ap()
    negi_d = nc.dram_tensor("negi", [8, 8], F32, kind="ExternalInput").ap()
    ones_d = nc.dram_tensor("ones", [128, 128], F32, kind="ExternalInput").ap()
    outT = nc.dram_tensor("outT", [D, S], F32, kind="ExternalOutput").ap()
    wbf = nc.dram_tensor("wbf", [L, 128, ELP], BF16).ap()
    kw = dict(kind="ExternalOutput") if debug else {}
    xs = nc.dram_tensor("xs", [D, S], F32, **kw).ap()
    yrT = nc.dram_tensor("yrT", [D, S], BF16, **kw).ap()
    atT = nc.dram_tensor("atT", [D, S], BF16, **kw).ap()
    sgT = nc.dram_tensor("sgT", [2 * D, S], BF16, **kw).ap()
    qT = nc.dram_tensor("qT", [NH, 128, S], BF16, **kw).ap()
    kT = nc.dram_tensor("kT", [NH, 128, S], BF16, **kw).ap()
    vv = nc.dram_tensor("vv", [S, D], BF16, **kw).ap()
    Dd = nc.dram_tensor("Dd", [NH, S], F32, **kw).ap()
    xdbg = [nc.dram_tensor("xd%d" % i, [D, S], F32, **kw).ap() for i in range(3)] if debug else None

    es = ExitStack()

    def sb(name, shape, dt):
        return es.enter_context(nc.sbuf_tensor(name, shape, dt))

    with es:
        X = sb("X", [128, 8, TT], F32)
        H = sb("H", [128, 8, TT], BF16)
        AFt = sb("AF", [128, NF, TT], BF16)
        SQ = sb("SQ", [128, 8, TT], BF16)
        WBall = sb("WB", [128, NBW, 2816], BF16)
        WB = [WBall[:, i, :] for i in range(NBW)]
        VEC = sb("VEC", [128, L * NVL + 8], F32)
        CL = sb("CL", [128, L * 8], F32)
        NFB = sb("NFB", [128, L], F32)
        TMPV = sb("TMPV", [128, L * 8], F32)
        NEGI = sb("NEGI", [8, 8], F32)
        ONES = sb("ONES", [128, 128], BF16)
        ONE8 = sb("ONE8", [8, TT], F32)
        RSTD = sb("RSTD", [128, 2, TT], F32)
        STD = sb("STD", [128, 2, TT], F32)
        SG = sb("SG", [128, 2, TT], F32)
        HL = sb("HL", [128, 8], F32)
        DL = sb("DL", [8, 1], F32)
        OB = sb("OB", [128, 4, TT], BF16)
        FE = sb("FE", [8, TT], F32)
        FD = sb("FD", [8, TT], F32)
        NDK = sb("NDK", [128, NKT, 8], F32)
        NDKH = sb("NDKH", [128, NKT], F32)
        NFA = 8 * (TT + 3) + 18 * TT
        FA = sb("FA", [128, NFA], F32)
        RXH = FA[:, 0:8 * (TT + 3)].rearrange("p (c t) -> p c t", t=TT + 3)
        _fo = 8 * (TT + 3)

        def fa2(i):
            return FA[:, _fo + i * 2 * TT:_fo + (i + 1) * 2 * TT].rearrange("p (b t) -> p b t", t=TT)
        T1, XC, RR, II, AA, MM, UU, HS, GL = [fa2(i) for i in range(9)]
        MASK = FA[:, 0:4 * TT].rearrange("p (r t) -> p r t", t=TT)
        M1, M2, OF = fa2(0), fa2(1), fa2(2)
        CB = sb("CB", [128, 42, TT], BF16)
        RGWt = CB[:, 0:4, :].rearrange("p f t -> p (f t)")
        XCB = CB[:, 4:6, :]
        VO = sb("VO", [128, 2, 256], BF16)
        YRt, ATt, SGt, PTt, MG = CB[:, 0:8, :], CB[:, 8:16, :], CB[:, 16:32, :], CB[:, 32:34, :], CB[:, 34:42, :]
        KTb = AFt[:, :, :].rearrange("p f t -> p (f t)")[:, 0:S]
        VHb = WBall[:, :, :].rearrange("p f t -> p (f t)")[:, 0:NKT * 128].rearrange("p (j d) -> p j d", d=128)
        QTb, PPb = H[:, 0:2, :], H[:, 2:6, :]
        KT4 = KTb.rearrange("p (n f d) -> p n f d", f=4, d=128)
        VH4 = VHb.rearrange("p (n f) d -> p n f d", f=4)
        NDKH4 = NDKH[:, :].rearrange("p (n f) -> p n f", f=4)
        DQb, RLt = X[:, 0:2, :], X[:, 4:6, :]
        TTl = [X[:, 2, :], X[:, 3, :], X[:, 6, :], X[:, 7, :]]
        PS = es.enter_context(nc.psum_tensor("PS", [128, 8, TT], F32))
        es.enter_context(nc.Block())
        kb = KB(nc, es)
        op, dma = kb.op, kb.dma

        rot = {}

        def nxt(name, n):
            i = rot.get(name, 0)
            rot[name] = (i + 1) % n
            return i

        def bank():
            b = nxt("ps", 8)
            return ("ps", b), PS[:, b, :]

        ncast_l = ELP // CASTW
        for l in range(L):
            for c in range(ncast_l):
                dma("POOL", lambda e, l=l, c=c: e.dma_start(out=wbf[l, :, c * CASTW:(c + 1) * CASTW],
                                                             in_=ws[l, :, c * CASTW:(c + 1) * CASTW]), w=[("D", "wbf")])
        dma("SP", lambda e: e.dma_start(out=VEC[:], in_=vecs_d[:, :]), w=["VEC"])
        dma("SP", lambda e: e.dma_start(out=NEGI[:], in_=negi_d[:, :]), w=["NEGI"])
        dma("POOL", lambda e: e.dma_start(out=ONES[:], in_=ones_d[:, :]), w=["ONES"])
        op("DVE", lambda e: e.memset(ONE8[:], 1.0), w=["ONE8"])
        op("DVE", lambda e: e.memset(FA[:], 0.0), w=["RXHall"])
        for l in range(L):
            o = l * NVL
            lc = o + VOFF["rg_lambda"]
            op("ACT", lambda e: e.activation(out=TMPV[:, l * 8:(l + 1) * 8], in_=VEC[:, lc:lc + 8], func=AF.Exp, scale=-1.0),
               r=["VEC"], w=["TMPV"])
            op("ACT", lambda e: e.activation(out=TMPV[:, l * 8:(l + 1) * 8], in_=TMPV[:, l * 8:(l + 1) * 8], func=AF.Ln, bias=1.0),
               r=["TMPV"], w=["TMPV"])
            op("ACT", lambda e: e.mul(out=CL[:, l * 8:(l + 1) * 8], in_=TMPV[:, l * 8:(l + 1) * 8], mul=-8.0), r=["TMPV"], w=["CL"])
            fc = o + VOFF["f_b"]
            op("ACT", lambda e: e.mul(out=NFB[:, l:l + 1], in_=VEC[:, fc:fc + 1], mul=-1.0), r=["VEC"], w=["NFB"])
        kb.reset()

        wst = {"seq": [], "issued": 0, "used": 0, "l": 0}

        def wbegin(l, first, count):
            wst.update(seq=list(range(first, first + count)), issued=0, used=0, l=l)

        def wnext(E):
            seq = wst["seq"]
            while wst["issued"] < min(wst["used"] + NBW, len(seq)):
                b = seq[wst["issued"]]
                sl = wst["issued"] % NBW
                Eb = BLOCK_E[b]
                off = int(BLOCK_OFF[b])
                l = wst["l"]
                dma("SP", lambda e, sl=sl, Eb=Eb, off=off, l=l: e.dma_start(out=WB[sl][:, 0:Eb], in_=wbf[l, :, off:off + Eb]),
                    r=[("D", "wbf")], w=[("W", sl)])
                wst["issued"] += 1
            b = seq[wst["used"]]
            assert BLOCK_E[b] == E, (b, BLOCK_E[b], E)
            sl = wst["used"] % NBW
            wst["used"] += 1
            return ("W", sl), WB[sl]

        def wend():
            assert wst["used"] == len(wst["seq"]), (wst["used"], len(wst["seq"]))

        def mm_group(pk, pp, lhs_list, rhs_list, rk):
            n = len(lhs_list)
            for k in range(n):
                ins = op("PE", lambda e, k=k: e.matmul(pp, lhsT=lhs_list[k], rhs=rhs_list[k], start=(k == 0), stop=(k == n - 1)),
                         r=rk[k], w=[pk])
            return ins

        def rms(gcol, out_final=None, tsl=None):
            for c in range(8):
                op("ACT", lambda e, c=c: e.activation(out=SQ[:, c, :], in_=X[:, c, :], func=AF.Square), r=[("X", c)], w=[("SQ", c)])
            qk_, qp = bank()
            mm_group(qk_, qp, [ONES[:]] * 8, [SQ[:, c, :] for c in range(8)], [[("SQ", c), "ONES"] for c in range(8)])
            s = nxt("std", 2)
            op("ACT", lambda e: e.activation(out=STD[:, s, :], in_=qp, func=AF.Sqrt, scale=1.0 / D, bias=EPS), r=[qk_], w=[("STD", s)])
            op("DVE", lambda e: e.reciprocal(out=RSTD[:, s, :], in_=STD[:, s, :]), r=[("STD", s)], w=[("RSTD", s)])
            for c in range(8):
                if out_final is None:
                    op("DVE", lambda e, c=c: e.scalar_tensor_tensor(out=H[:, c, :], in0=X[:, c, :], scalar=VEC[:, gcol + c:gcol + c + 1],
                                                                    in1=RSTD[:, s, :], op0=ALU.mult, op1=ALU.mult),
                       r=[("X", c), ("RSTD", s)], w=[("H", c)])
                else:
                    o2 = nxt("of", 2)
                    op("DVE", lambda e, c=c: e.scalar_tensor_tensor(out=OF[:, o2, :], in0=X[:, c, :], scalar=VEC[:, gcol + c:gcol + c + 1],
                                                                    in1=RSTD[:, s, :], op0=ALU.mult, op1=ALU.mult),
                       r=[("X", c), ("RSTD", s)], w=[("OF", o2)])
                    dma("POOL", lambda e, c=c: e.dma_start(out=outT[c * 128:(c + 1) * 128, tsl], in_=OF[:, o2, :]), r=[("OF", o2)], w=[("D", "outT")])

        def ffn():
            for f in range(NF):
                wk, wb = wnext(2048)
                gk, gp = bank()
                mm_group(gk, gp, [wb[:, k * 128:(k + 1) * 128] for k in range(8)], [H[:, k, :] for k in range(8)],
                         [[wk, ("H", k)] for k in range(8)])
                uk, up = bank()
                mm_group(uk, up, [wb[:, 1024 + k * 128:1024 + (k + 1) * 128] for k in range(8)], [H[:, k, :] for k in range(8)],
                         [[wk, ("H", k)] for k in range(8)])
                s = nxt("sg", 2)
                op("ACT", lambda e: e.activation(out=SG[:, s, :], in_=gp, func=AF.Silu), r=[gk], w=[("SG", s)])
                op("DVE", lambda e: e.tensor_tensor(out=AFt[:, f, :], in0=up, in1=SG[:, s, :], op=ALU.mult), r=[uk, ("SG", s)], w=[("AF", f)])
            for d in range(8):
                wk, wb = wnext(2816)
                yk, yp = bank()
                mm_group(yk, yp, [wb[:, f * 128:(f + 1) * 128] for f in range(NF)], [AFt[:, f, :] for f in range(NF)],
                         [[wk, ("AF", f)] for f in range(NF)])
                op("DVE", lambda e: e.scalar_tensor_tensor(out=X[:, d, :], in0=yp, scalar=0.5, in1=X[:, d, :], op0=ALU.mult, op1=ALU.add),
                   r=[yk, ("X", d)], w=[("X", d)])

        def proj_chunk(wk, wb, woff, nk=8, rhs=None, rkeys=None):
            pk, pp = bank()
            rhs = rhs or [H[:, k, :] for k in range(nk)]
            rkeys = rkeys or [("H", k) for k in range(nk)]
            mm_group(pk, pp, [wb[:, woff + k * 128:woff + (k + 1) * 128] for k in range(nk)], rhs, [[wk, rkeys[k]] for k in range(nk)])
            return pk, pp

        xview = lambda t: t.rearrange("(c p) t -> p c t", p=128)

        for l in range(L):
            vo = l * NVL
            xsrc = xin if l == 0 else xs
            op("DVE", lambda e: e.memset(HL[:], 0.0), w=["HL"])
            op("DVE", lambda e: e.memset(DL[:], 0.0), w=["DL"])
            op("DVE", lambda e: e.memset(RXH[:, :, 0:3], 0.0), w=["RXHall"])
            for it in range(NT):
                tsl = tsl_(it, TT)
                wbegin(l, 0, N_A)
                dma("SP", lambda e: e.dma_start(out=X[:], in_=xview(xsrc)[:, :, tsl]), r=[("D", "xs")], w=[("X", c) for c in range(8)])
                rms(vo + VOFF["ffn1_norm"])
                ffn()
                dma("POOL", lambda e: e.dma_start(out=xview(xs)[:, :, tsl], in_=X[:]), r=[("X", c) for c in range(8)], w=[("D", "xs")])
                rms(vo + VOFF["mix_norm"])
                wk, wb = wnext(2048)
                op("DVE", lambda e: e.tensor_copy(out=RGWt[:], in_=wb[:, 0:2048]), r=[wk], w=["RGW"])
                cw = vo + VOFF["conv_w"]
                cbc = vo + VOFF["conv_b"]
                def rnn_stage1(c):
                    b2 = c % 2
                    wk, wb = wnext(2048)
                    rxk, rxp = proj_chunk(wk, wb, 0)
                    gtk, gtp = proj_chunk(wk, wb, 1024)
                    op("ACT", lambda e: e.activation(out=RXH[:, c, 3:TT + 3], in_=rxp, func=AF.Copy), r=[rxk], w=[("RXH", c)])
                    op("DVE", lambda e: e.tensor_scalar(out=T1[:, b2, :], in0=RXH[:, c, 0:TT], scalar1=VEC[:, cw + c:cw + c + 1],
                                                        scalar2=VEC[:, cbc + c:cbc + c + 1], op0=ALU.mult, op1=ALU.add),
                       r=[("RXH", c)], w=[("T1", b2)])
                    for kk in (1, 2):
                        op("DVE", lambda e, kk=kk: e.scalar_tensor_tensor(out=T1[:, b2, :], in0=RXH[:, c, kk:TT + kk],
                                                                          scalar=VEC[:, cw + 8 * kk + c:cw + 8 * kk + c + 1],
                                                                          in1=T1[:, b2, :], op0=ALU.mult, op1=ALU.add),
                           r=[("RXH", c)], w=[("T1", b2)])
                    op("DVE", lambda e: e.scalar_tensor_tensor(out=XC[:, b2, :], in0=RXH[:, c, 3:TT + 3], scalar=VEC[:, cw + 24 + c:cw + 25 + c],
                                                               in1=T1[:, b2, :], op0=ALU.mult, op1=ALU.add),
                       r=[("RXH", c), ("T1", b2)], w=[("XC", b2)])
                    op("DVE", lambda e: e.tensor_copy(out=RXH[:, c, 0:3], in_=RXH[:, c, TT:TT + 3]), r=[("RXH", c)], w=[("RXH", c)])
                    op("DVE", lambda e: e.tensor_copy(out=XCB[:, b2, :], in_=XC[:, b2, :]), r=[("XC", b2)], w=[("XCB", b2)])
                    return gtk, gtp

                def rnn_stage2(c, gtk, gtp):
                    b2 = c % 2
                    rk_, rp = bank()
                    op("PE", lambda e: e.matmul(rp, lhsT=RGWt[:, c * 256:c * 256 + 128], rhs=XCB[:, b2, :], start=True, stop=True),
                       r=["RGW", ("XCB", b2)], w=[rk_])
                    ik_, ip = bank()
                    op("PE", lambda e: e.matmul(ip, lhsT=RGWt[:, c * 256 + 128:c * 256 + 256], rhs=XCB[:, b2, :], start=True, stop=True),
                       r=["RGW", ("XCB", b2)], w=[ik_])
                    bac = vo + VOFF["rg_ba"] + c
                    bxc = vo + VOFF["rg_bx"] + c
                    op("ACT", lambda e: e.activation(out=RR[:, b2, :], in_=rp, func=AF.Sigmoid, bias=VEC[:, bac:bac + 1]), r=[rk_], w=[("RR", b2)])
                    op("ACT", lambda e: e.activation(out=AA[:, b2, :], in_=RR[:, b2, :], func=AF.Exp, scale=CL[:, l * 8 + c:l * 8 + c + 1]),
                       r=[("RR", b2)], w=[("AA", b2)])
                    op("ACT", lambda e: e.activation(out=II[:, b2, :], in_=ip, func=AF.Sigmoid, bias=VEC[:, bxc:bxc + 1]), r=[ik_], w=[("II", b2)])
                    op("ACT", lambda e: e.activation(out=MM[:, b2, :], in_=AA[:, b2, :], func=AF.Square), r=[("AA", b2)], w=[("MM", b2)])
                    op("ACT", lambda e: e.activation(out=MM[:, b2, :], in_=MM[:, b2, :], func=AF.Sqrt, scale=-1.0, bias=1.0), r=[("MM", b2)], w=[("MM", b2)])
                    op("ACT", lambda e: e.activation(out=GL[:, b2, :], in_=gtp, func=AF.Gelu_apprx_tanh), r=[gtk], w=[("GL", b2)])
                    op("DVE", lambda e: e.tensor_tensor(out=UU[:, b2, :], in0=II[:, b2, :], in1=XC[:, b2, :], op=ALU.mult),
                       r=[("II", b2), ("XC", b2)], w=[("UU", b2)])
                    op("DVE", lambda e: e.tensor_tensor(out=UU[:, b2, :], in0=UU[:, b2, :], in1=MM[:, b2, :], op=ALU.mult),
                       r=[("MM", b2)], w=[("UU", b2)])
                    op("DVE", lambda e: e.scalar_tensor_tensor(out=UU[:, b2, 0:1], in0=AA[:, b2, 0:1], scalar=HL[:, c:c + 1], in1=UU[:, b2, 0:1],
                                                               op0=ALU.mult, op1=ALU.add),
                       r=[("AA", b2), "HL"], w=[("UU", b2)])
                    op("DVE", lambda e: e.tensor_tensor_scan(out=HS[:, b2, :], data0=AA[:, b2, :], data1=UU[:, b2, :], initial=0.0,
                                                             op0=ALU.mult, op1=ALU.add),
                       r=[("AA", b2), ("UU", b2)], w=[("HS", b2)])
                    op("DVE", lambda e: e.tensor_copy(out=HL[:, c:c + 1], in_=HS[:, b2, TT - 1:TT]), r=[("HS", b2)], w=["HL"])
                    o4 = nxt("ob", 4)
                    op("DVE", lambda e: e.tensor_tensor(out=OB[:, o4, :], in0=HS[:, b2, :], in1=GL[:, b2, :], op=ALU.mult),
                       r=[("HS", b2), ("GL", b2)], w=[("OB", o4)])
                    dma("POOL", lambda e: e.dma_start(out=yrT[c * 128:(c + 1) * 128, tsl], in_=OB[:, o4, :]), r=[("OB", o4)], w=[("D", "yrT")])

                pend = rnn_stage1(0)
                for c in range(8):
                    nxt_pend = rnn_stage1(c + 1) if c + 1 < 8 else None
                    rnn_stage2(c, *pend)
                    pend = nxt_pend

                for h in range(NH):
                    wk, wb = wnext(2048)
                    for qk in range(2):
                        pk, pp = proj_chunk(wk, wb, qk * 1024)
                        op("ACT", lambda e: e.activation(out=SQ[:, qk, :], in_=pp, func=AF.Square), r=[pk], w=[("SQ", qk)])
                        sk_, sp_ = bank()
                        op("PE", lambda e: e.matmul(sp_, lhsT=ONES[:], rhs=SQ[:, qk, :], start=True, stop=True), r=["ONES", ("SQ", qk)], w=[sk_])
                        s = nxt("std", 2)
                        if qk == 0:
                            op("ACT", lambda e: e.activation(out=STD[:, s, :], in_=sp_, func=AF.Sqrt, scale=1.0, bias=128.0 * EPS), r=[sk_], w=[("STD", s)])
                        else:
                            op("ACT", lambda e: e.activation(out=STD[:, s, :], in_=sp_, func=AF.Sqrt, scale=1.0 / 128.0, bias=EPS), r=[sk_], w=[("STD", s)])
                        op("DVE", lambda e: e.reciprocal(out=RSTD[:, s, :], in_=STD[:, s, :]), r=[("STD", s)], w=[("RSTD", s)])
                        o4 = nxt("ob", 4)
                        gcol = vo + VOFF["q_norm" if qk == 0 else "k_norm"]
                        op("DVE", lambda e: e.scalar_tensor_tensor(out=OB[:, o4, :], in0=pp, scalar=VEC[:, gcol:gcol + 1], in1=RSTD[:, s, :],
                                                                   op0=ALU.mult, op1=ALU.mult),
                           r=[pk, ("RSTD", s)], w=[("OB", o4)])
                        dst = qT if qk == 0 else kT
                        dma("POOL", lambda e, dst=dst: e.dma_start(out=dst[h, :, tsl], in_=OB[:, o4, :]), r=[("OB", o4)], w=[("D", "qk")])
                for qv in range(4):
                    wk, wb = wnext(2048)
                    for t4 in range(4):
                        pk, pp = bank()
                        mm_group(pk, pp[:, 0:256], [H[:, k, t4 * 128:(t4 + 1) * 128] for k in range(8)],
                                 [wb[:, k * 256:(k + 1) * 256] for k in range(8)], [[wk, ("H", k)] for k in range(8)])
                        v2 = nxt("vo", 2)
                        op("DVE", lambda e: e.tensor_copy(out=VO[:, v2, :], in_=pp[:, 0:256]), r=[pk], w=[("VO", v2)])
                        dma("POOL", lambda e: e.dma_start(out=vv.rearrange("(n f p) c -> n f p c", f=4, p=128)[it, t4, :, qv * 256:(qv + 1) * 256], in_=VO[:, v2, :]),
                            r=[("VO", v2)], w=[("D", "vv")])
                wk, wb = wnext(64)
                fk, fp = bank()
                mm_group(fk, fp[0:8, :], [wb[:, k * 8:(k + 1) * 8] for k in range(8)], [H[:, k, :] for k in range(8)],
                         [[wk, ("H", k)] for k in range(8)])
                op("ACT", lambda e: e.activation(out=FE[:], in_=fp[0:8, :], func=AF.Exp, scale=-1.0, bias=NFB[0:8, l:l + 1]), r=[fk], w=["FE"])
                op("ACT", lambda e: e.activation(out=FE[:], in_=FE[:], func=AF.Ln, bias=1.0), r=["FE"], w=["FE"])
                op("ACT", lambda e: e.mul(out=FE[:], in_=FE[:], mul=-1.0), r=["FE"], w=["FE"])
                op("DVE", lambda e: e.tensor_tensor(out=FE[:, 0:1], in0=FE[:, 0:1], in1=DL[:, 0:1], op=ALU.add), r=["FE", "DL"], w=["FE"])
                op("DVE", lambda e: e.tensor_tensor_scan(out=FD[:], data0=ONE8[:], data1=FE[:], initial=0.0, op0=ALU.mult, op1=ALU.add),
                   r=["FE", "ONE8"], w=["FD"])
                op("DVE", lambda e: e.tensor_copy(out=DL[:, 0:1], in_=FD[:, TT - 1:TT]), r=["FD"], w=["DL"])
                dma("POOL", lambda e: e.dma_start(out=Dd[:, tsl], in_=FD[:]), r=["FD"], w=[("D", "Dd")])
                tk, tp = bank()
                for j4 in range(4):
                    op("PE", lambda e, j4=j4: e.matmul(tp[:, j4 * 8:(j4 + 1) * 8], lhsT=FD[:, j4 * 128:(j4 + 1) * 128], rhs=NEGI[:], start=True, stop=True),
                       r=["FD", "NEGI"], w=[tk])
                op("ACT", lambda e: e.activation(out=NDK[:, tsl_(it, 4), :], in_=tp[:, 0:32].rearrange("p (a b) -> p a b", b=8), func=AF.Copy),
                   r=[tk], w=["NDK"])
                for c in range(8):
                    wk, wb = wnext(2048)
                    for ab in range(2):
                        pk, pp = proj_chunk(wk, wb, ab * 1024)
                        o4 = nxt("ob", 4)
                        mcol = vo + VOFF["merge_b"] + ab * 8 + c
                        op("ACT", lambda e: e.activation(out=OB[:, o4, :], in_=pp, func=AF.Sigmoid, bias=VEC[:, mcol:mcol + 1]), r=[pk], w=[("OB", o4)])
                        row = (ab * 8 + c) * 128
                        dma("POOL", lambda e: e.dma_start(out=sgT[row:row + 128, tsl], in_=OB[:, o4, :]), r=[("OB", o4)], w=[("D", "sgT")])
                wend()
            kb.reset()
            if stop == ("A", l):
                break
            dma("SP", lambda e: e.dma_start(out=MASK, in_=mask_d.rearrange("r p c -> p r c")), w=["MASK"])
            for hh in range(NH):
                dma("SP", lambda e: e.dma_start(out=KTb[:], in_=kT[hh, :, :]), r=[("D", "qk")], w=["KT"])
                dma("SP", lambda e: e.dma_start(out=VHb[:], in_=vv.rearrange("(j p) c -> p j c", p=128)[:, :, tsl_(hh, 128)]), r=[("D", "vv")], w=["VH"])
                op("DVE", lambda e: e.tensor_copy(out=NDKH[:], in_=NDK[:, :, tsl_(hh, 1)].rearrange("p j o -> p (j o)")), r=["NDK"], w=["NDKH"])
                for i in range(NT):
                    q2 = nxt("qt", 2)
                    dma("SP", lambda e: e.dma_start(out=QTb[:, q2, :], in_=qT[hh, :, i * TT:(i + 1) * TT]), r=[("D", "qk")], w=[("QT", q2)])
                    dma("SP", lambda e: e.dma_start(out=DQb[:, q2, :], in_=Dd[tsl_(hh, 1), i * TT:(i + 1) * TT].partition_broadcast(128)),
                        r=[("D", "Dd")], w=[("DQ", q2)])
                    ab_ = nxt("acc", 2)
                    ok_, opp = ("ps", 2 * ab_), PS[:, 2 * ab_, :]
                    lk_, lpp = ("ps", 2 * ab_ + 1), PS[:, 2 * ab_ + 1, :]

                    def qk_phase(g_, t_, r):
                        sb_ = 4 + nxt("sbank", 4)
                        sk_, spp = ("ps", sb_), PS[:, sb_, :]
                        op("PE", lambda e: e.matmul(spp, lhsT=KT4[:, g_, t_, :], rhs=QTb[:, q2, :], start=True, stop=True), r=["KT", ("QT", q2)], w=[sk_])
                        t2 = nxt("tt", 4)
                        op("DVE", lambda e: e.tensor_tensor(out=TTl[t2], in0=spp, in1=DQb[:, q2, :], op=ALU.add), r=[sk_, ("DQ", q2)], w=[("TT", t2)])
                        if r is not None:
                            op("DVE", lambda e: e.tensor_tensor(out=TTl[t2], in0=TTl[t2], in1=MASK[:, r, :], op=ALU.add),
                               r=["MASK", ("TT", t2)], w=[("TT", t2)])
                        p2 = nxt("pp", 4)
                        op("ACT", lambda e: e.activation(out=PPb[:, p2, :], in_=TTl[t2], func=AF.Exp, bias=NDKH4[:, g_, t_:t_ + 1]), r=[("TT", t2), "NDKH"], w=[("PP", p2)])
                        return p2

                    def pv_phase(g_, t_, p2, first, last):
                        op("PE", lambda e: e.matmul(opp, lhsT=VH4[:, g_, t_, :],
                                                    rhs=PPb[:, p2, :], start=first, stop=last), r=["VH", ("PP", p2)], w=[ok_])
                        op("PE", lambda e: e.matmul(lpp, lhsT=ONES[:], rhs=PPb[:, p2, :], start=first, stop=last), r=["ONES", ("PP", p2)], w=[lk_])

                    tiles = [(i, 0, 0)] + [(g, t, None) for g in range(i) for t in range(4)] + [(i, r_, r_) for r_ in (1, 2, 3)]
                    LOOK = 2
                    pbuf = {}
                    for k in range(min(LOOK, len(tiles))):
                        pbuf[k] = qk_phase(*tiles[k])
                    for k in range(len(tiles)):
                        if k + LOOK < len(tiles):
                            pbuf[k + LOOK] = qk_phase(*tiles[k + LOOK])
                        pv_phase(tiles[k][0], tiles[k][1], pbuf.pop(k), k == 0, k == len(tiles) - 1)
                    r2 = nxt("rl", 2)
                    op("DVE", lambda e: e.reciprocal(out=RLt[:, r2, :], in_=lpp), r=[lk_], w=[("RL", r2)])
                    o4 = nxt("ob", 4)
                    op("DVE", lambda e: e.tensor_tensor(out=OB[:, o4, :], in0=opp, in1=RLt[:, r2, :], op=ALU.mult), r=[ok_, ("RL", r2)], w=[("OB", o4)])
                    dma("POOL", lambda e: e.dma_start(out=atT[tsl_(hh, 128), i * TT:(i + 1) * TT], in_=OB[:, o4, :]), r=[("OB", o4)], w=[("D", "atT")])
            kb.reset()
            if stop == ("B", l):
                break
            for it in range(NT):
                tsl = tsl_(it, TT)
                wbegin(l, N_A, len(BLOCK_E) - N_A)
                dma("SP", lambda e: e.dma_start(out=X[:], in_=xview(xs)[:, :, tsl]), r=[("D", "xs")], w=[("X", c) for c in range(8)])
                dma("SP", lambda e: e.dma_start(out=YRt[:], in_=xview(yrT)[:, :, tsl]), r=[("D", "yrT")], w=["YR"])
                dma("SP", lambda e: e.dma_start(out=ATt[:], in_=xview(atT)[:, :, tsl]), r=[("D", "atT")], w=["AT"])
                dma("SP", lambda e: e.dma_start(out=SGt[:], in_=xview(sgT)[:, :, tsl]), r=[("D", "sgT")], w=["SGT"])
                dma("POOL", lambda e: e.dma_start(out=PTt[:], in_=pin[l].rearrange("(c p) t -> p c t", p=128)[:, :, tsl]), w=["PT"])
                for d in range(8):
                    wk, wb = wnext(2048)
                    ak, ap_ = proj_chunk(wk, wb, 0, rhs=[YRt[:, k, :] for k in range(8)], rkeys=["YR"] * 8)
                    bk, bp_ = proj_chunk(wk, wb, 1024, rhs=[ATt[:, k, :] for k in range(8)], rkeys=["AT"] * 8)
                    m2 = nxt("m", 2)
                    op("DVE", lambda e: e.tensor_tensor(out=M1[:, m2, :], in0=ap_, in1=SGt[:, d, :], op=ALU.mult), r=[ak, "SGT"], w=[("M1", m2)])
                    op("DVE", lambda e: e.tensor_tensor(out=M2[:, m2, :], in0=bp_, in1=SGt[:, 8 + d, :], op=ALU.mult), r=[bk, "SGT"], w=[("M2", m2)])
                    op("DVE", lambda e: e.tensor_tensor(out=MG[:, d, :], in0=M1[:, m2, :], in1=M2[:, m2, :], op=ALU.add), r=[("M1", m2), ("M2", m2)], w=[("MG", d)])
                for dp in range(4):
                    wk, wb = wnext(2048)
                    for s2 in range(2):
                        d = 2 * dp + s2
                        pk, pp = proj_chunk(wk, wb, s2 * 1024, rhs=[MG[:, k, :] for k in range(8)], rkeys=[("MG", k) for k in range(8)])
                        op("DVE", lambda e: e.tensor_tensor(out=X[:, d, :], in0=pp, in1=X[:, d, :], op=ALU.add), r=[pk, ("X", d)], w=[("X", d)])
                if debug:
                    dma("POOL", lambda e: e.dma_start(out=xview(xdbg[0])[:, :, tsl], in_=X[:]), r=[("X", c) for c in range(8)])
                rms(vo + VOFF["ffn2_norm"])
                ffn()
                if debug:
                    dma("POOL", lambda e: e.dma_start(out=xview(xdbg[1])[:, :, tsl], in_=X[:]), r=[("X", c) for c in range(8)])
                rms(vo + VOFF["ple_norm"])
                for dp in range(4):
                    wk, wb = wnext(2048)
                    gm = []
                    for s2 in range(2):
                        d = 2 * dp + s2
                        pk, pp = proj_chunk(wk, wb, s2 * 1024)
                        bcol = vo + VOFF["ple_b_gate"] + d
                        m2 = nxt("m", 2)
                        op("ACT", lambda e: e.activation(out=M1[:, m2, :], in_=pp, func=AF.Sigmoid, bias=VEC[:, bcol:bcol + 1]), r=[pk], w=[("M1", m2)])
                        gm.append(m2)
                    wk2, wb2 = wnext(512)
                    for s2 in range(2):
                        d = 2 * dp + s2
                        m2 = gm[s2]
                        ek, ep = proj_chunk(wk2, wb2, s2 * 256, nk=2, rhs=[PTt[:, k, :] for k in range(2)], rkeys=["PT"] * 2)
                        op("DVE", lambda e: e.tensor_tensor(out=M2[:, m2, :], in0=ep, in1=M1[:, m2, :], op=ALU.mult), r=[ek, ("M1", m2)], w=[("M2", m2)])
                        op("DVE", lambda e: e.tensor_tensor(out=X[:, d, :], in0=M2[:, m2, :], in1=X[:, d, :], op=ALU.add), r=[("M2", m2), ("X", d)], w=[("X", d)])
                wend()
                if debug:
                    dma("POOL", lambda e: e.dma_start(out=xview(xdbg[2])[:, :, tsl], in_=X[:]), r=[("X", c) for c in range(8)])
                if l == L - 1:
                    rms(L * NVL, out_final=True, tsl=tsl)
                else:
                    dma("POOL", lambda e: e.dma_start(out=xview(xs)[:, :, tsl], in_=X[:]), r=[("X", c) for c in range(8)], w=[("D", "xs")])
            kb.reset()
        kb.softbar()
    return nc


_PROG = {}


def get_program(S, L, stop=None):
    key = (S, L, stop)
    if key not in _PROG:
        _PROG[key] = build_program(S, L, stop)
    return _PROG[key]


def kernel(**inputs):
    inp = {k: np.asarray(v) for k, v in inputs.items()}
    B, S, _ = inp["x"].shape
    L = inp["ffn1_w_in"].shape[0]
    nc = get_program(S, L)
    shared = prep_shared(inp, L)
    in_maps = []
    for core in range(8):
        b = core % B
        m = dict(shared)
        m["xT"] = np.ascontiguousarray(inp["x"][b].T)
        m["pT"] = np.ascontiguousarray(inp["p"][:, b].transpose(0, 2, 1))
        in_maps.append(m)
    res = run_bass_kernel_spmd(nc, in_maps, core_ids=list(range(8)))
    out = np.stack([np.ascontiguousarray(res.results[b]["outT"].T) for b in range(B)], axis=0)
    return out.astype(np.float32)
```
